# Optimizing a Trainium2 kernel written in Bass

```python
import jax, jax.numpy as jnp
from jax import lax
import numpy as np

D_MODEL = 1024
BATCH = 32
SEQ = 256
DEPTH = 4
DEC_BATCH = 2
DEC_SEQ = 2048
PAST_LEN = 512

GRID_W = 64
HEAD_DIM = 64
ATTN_WIDTH = D_MODEL // 2
N_HEADS = ATTN_WIDTH // HEAD_DIM
N_KV_HEADS = 2
GROUP = N_HEADS // N_KV_HEADS
KV_WIDTH = N_KV_HEADS * HEAD_DIM
WINDOW = 128
BLOCK = 128
GM_WIDTH = D_MODEL // 4
GM_HEADS = 4
GM_DIM = GM_WIDTH // GM_HEADS
CHUNK = 128
POOL_WIDTH = D_MODEL // 4
POOL_WINDOWS = (2, 4, 8, 16)
POOL_DIM = POOL_WIDTH // len(POOL_WINDOWS)
MIX_WIDTH = ATTN_WIDTH + GM_WIDTH + POOL_WIDTH
IN_WIDTH = ATTN_WIDTH + 2 * KV_WIDTH + 2 * GM_WIDTH + POOL_WIDTH
SPLITS = (ATTN_WIDTH, ATTN_WIDTH + KV_WIDTH, ATTN_WIDTH + 2 * KV_WIDTH,
          ATTN_WIDTH + 2 * KV_WIDTH + GM_WIDTH, ATTN_WIDTH + 2 * KV_WIDTH + 2 * GM_WIDTH)
D_FF = 2816
N_MOD = 9
EPS = 1e-6
ROPE_BASE = 10000.0
NEG_INF = -1e30
ATTN_SCALE = HEAD_DIM ** -0.5

kernel_name = "hymba_macaron_prefix_dit_step"


def rms_norm(x, g):
    xf = x.astype(jnp.float32)
    y = xf * lax.rsqrt(jnp.mean(xf * xf, axis=-1, keepdims=True) + EPS)
    return (y * g.astype(jnp.float32)).astype(x.dtype)


def adaln(cvec, w_mod_l, b_mod_l):
    m = jax.nn.silu(cvec) @ w_mod_l + b_mod_l
    return m.reshape(cvec.shape[0], N_MOD, D_MODEL)


def modulate(h, mod, j):
    return h * (1 + mod[:, j + 1][:, None, :]) + mod[:, j][:, None, :]


def swiglu(h, w1, w2):
    g, u = jnp.split(h @ w1, 2, axis=-1)
    return (jax.nn.silu(g) * u) @ w2


def ffn_half(x, mod, j, g_pre, g_post, w1, w2):
    h = modulate(rms_norm(x, g_pre), mod, j)
    return x + 0.5 * mod[:, j + 2][:, None, :] * rms_norm(swiglu(h, w1, w2), g_post)


def axial_rope_tables(n_tok):
    rows = n_tok // GRID_W
    t = jnp.arange(rows * GRID_W, dtype=jnp.int32)
    row = (t // GRID_W).astype(jnp.float32)
    col = (t % GRID_W).astype(jnp.float32)
    nf = HEAD_DIM // 4
    inv = ROPE_BASE ** (-jnp.arange(nf, dtype=jnp.float32) / nf)
    ang = jnp.concatenate([row[:, None] * inv, col[:, None] * inv], axis=-1)
    return jnp.cos(ang), jnp.sin(ang)


def apply_axial_rope(x, cos, sin):
    S = x.shape[1]
    nf = HEAD_DIM // 4
    shape = (1, S) + (1,) * (x.ndim - 3) + (nf,)
    out = []
    for a in range(2):
        z = x[..., a * 2 * nf:(a + 1) * 2 * nf]
        z1, z2 = z[..., :nf], z[..., nf:]
        cs = cos[:, a * nf:(a + 1) * nf].reshape(shape).astype(x.dtype)
        sn = sin[:, a * nf:(a + 1) * nf].reshape(shape).astype(x.dtype)
        out += [z1 * cs - z2 * sn, z2 * cs + z1 * sn]
    return jnp.concatenate(out, axis=-1)


def sink_column(sink, lead_shape):
    s = sink.astype(jnp.float32).reshape(N_KV_HEADS, GROUP)[None, :, :, None, None]
    return jnp.broadcast_to(s, lead_shape + (1,))


def project(h, w_in):
    B, S, _ = h.shape
    q, k, v, gu, gv, pl = jnp.split(h @ w_in, SPLITS, axis=-1)
    q = q.reshape(B, S, N_KV_HEADS, GROUP, HEAD_DIM)
    k = k.reshape(B, S, N_KV_HEADS, HEAD_DIM)
    v = v.reshape(B, S, N_KV_HEADS, HEAD_DIM)
    return q, k, v, gu, gv, pl


def context_attention(q, k, v, sink):
    B, S = q.shape[:2]
    nblk = S // BLOCK
    qb = q.reshape(B, nblk, BLOCK, N_KV_HEADS, GROUP, HEAD_DIM).swapaxes(0, 1)

    def one(qblk):
        s = jnp.einsum('bqkgd,bskd->bkgqs', qblk, k).astype(jnp.float32) * ATTN_SCALE
        p = jax.nn.softmax(jnp.concatenate([s, sink_column(sink, s.shape[:-1])], axis=-1), axis=-1)
        return jnp.einsum('bkgqs,bskd->bqkgd', p[..., :-1].astype(v.dtype), v)

    o = lax.map(one, qb)
    return o.swapaxes(0, 1).reshape(B, S, ATTN_WIDTH)


def latent_attention(q, k, v, ck, cv, sink):
    B, S = q.shape[:2]
    nblk = S // BLOCK
    pad = ((0, 0), (BLOCK, BLOCK), (0, 0), (0, 0))
    kp = jnp.pad(k, pad)
    vp = jnp.pad(v, pad)
    qb = q.reshape(B, nblk, BLOCK, N_KV_HEADS, GROUP, HEAD_DIM).swapaxes(0, 1)
    qoff = jnp.arange(BLOCK, dtype=jnp.int32)
    koff = jnp.arange(3 * BLOCK, dtype=jnp.int32)
    n_band = 3 * BLOCK

    def one(args):
        i, qblk = args
        kb = lax.dynamic_slice_in_dim(kp, i * BLOCK, n_band, axis=1)
        vb = lax.dynamic_slice_in_dim(vp, i * BLOCK, n_band, axis=1)
        qpos = i * BLOCK + qoff
        kpos = i * BLOCK - BLOCK + koff
        valid = (kpos[None, :] >= 0) & (kpos[None, :] < S) & (jnp.abs(qpos[:, None] - kpos[None, :]) <= WINDOW)
        s_lat = jnp.einsum('bqkgd,bskd->bkgqs', qblk, kb).astype(jnp.float32) * ATTN_SCALE
        s_lat = jnp.where(valid, s_lat, NEG_INF)
        s_ctx = jnp.einsum('bqkgd,bckd->bkgqc', qblk, ck).astype(jnp.float32) * ATTN_SCALE
        p = jax.nn.softmax(jnp.concatenate([s_lat, s_ctx, sink_column(sink, s_lat.shape[:-1])], axis=-1), axis=-1)
        o = jnp.einsum('bkgqs,bskd->bqkgd', p[..., :n_band].astype(vb.dtype), vb)
        return o + jnp.einsum('bkgqc,bckd->bqkgd', p[..., n_band:-1].astype(cv.dtype), cv)

    o = lax.map(one, (jnp.arange(nblk, dtype=jnp.int32), qb))
    return o.swapaxes(0, 1).reshape(B, S, ATTN_WIDTH)


def chunk_gmlp(u, v, w_s, b_s):
    B, S, _ = v.shape
    n = S // CHUNK
    vh = v.reshape(B, n, CHUNK, GM_HEADS, GM_DIM).astype(jnp.float32)
    vh = (vh * lax.rsqrt(jnp.mean(vh * vh, axis=-1, keepdims=True) + EPS)).astype(v.dtype)
    z = jnp.einsum('hpq,bnqhd->bnphd', w_s, vh) + b_s.T[None, None, :, :, None]
    return u * z.reshape(B, S, GM_WIDTH)


def multiscale_pool(h, w_pool, scale):
    B, S, C = h.shape
    hf = h.astype(jnp.float32)
    cs = jnp.concatenate([jnp.zeros((B, 1, C), jnp.float32), jnp.cumsum(hf, axis=1)], axis=1)
    t = jnp.arange(S, dtype=jnp.int32)
    outs = []
    for g, w in enumerate(POOL_WINDOWS):
        lo = jnp.clip(t - w // 2, 0, S)
        hi = jnp.clip(t + w // 2, 0, S)
        cg = cs[..., g * POOL_DIM:(g + 1) * POOL_DIM]
        mean = (cg[:, hi] - cg[:, lo]) / (hi - lo).astype(jnp.float32)[None, :, None]
        d = (mean - hf[..., g * POOL_DIM:(g + 1) * POOL_DIM]).astype(h.dtype)
        outs.append(d @ w_pool[g])
    return jnp.concatenate(outs, axis=-1) * scale


def merge_heads(attn, gu, gv, pl, w_s, b_s, w_pool, pool_scale, w_out):
    a = chunk_gmlp(gu, gv, w_s, b_s)
    p = multiscale_pool(pl, w_pool, pool_scale)
    return jnp.concatenate([attn, a, p], axis=-1) @ w_out


def setup_inputs(seed: int = 0) -> dict:
    key = jax.random.key(seed)
    ks = jax.random.split(key, 18)
    f = jnp.float32
    nrm = lambda k, shape, s: jax.random.normal(k, shape, f) * s
    return {
        "x_prompt": nrm(ks[0], (BATCH, SEQ, D_MODEL), 1.0),
        "x_sample": nrm(ks[1], (DEC_BATCH, DEC_SEQ, D_MODEL), 1.0),
        "cache_k": nrm(ks[2], (DEC_BATCH, DEPTH, PAST_LEN, N_KV_HEADS, HEAD_DIM), 1.0),
        "cache_v": nrm(ks[3], (DEC_BATCH, DEPTH, PAST_LEN, N_KV_HEADS, HEAD_DIM), 1.0),
        "c": nrm(ks[4], (DEC_BATCH, D_MODEL), 1.0),
        "c_ctx": nrm(ks[5], (D_MODEL,), 1.0),
        "w_mod": nrm(ks[6], (DEPTH, D_MODEL, N_MOD * D_MODEL), 0.5 * D_MODEL ** -0.5),
        "b_mod": nrm(ks[7], (DEPTH, N_MOD * D_MODEL), 0.02),
        "norm_w": 1.0 + nrm(ks[8], (DEPTH, 6, D_MODEL), 0.02),
        "w_in": nrm(ks[9], (DEPTH, D_MODEL, IN_WIDTH), D_MODEL ** -0.5),
        "w_out": nrm(ks[10], (DEPTH, MIX_WIDTH, D_MODEL), MIX_WIDTH ** -0.5),
        "attn_sink": nrm(ks[11], (DEPTH, N_HEADS), 0.5),
        "w_spatial": nrm(ks[12], (DEPTH, GM_HEADS, CHUNK, CHUNK), CHUNK ** -0.5),
        "b_spatial": 1.0 + nrm(ks[13], (DEPTH, GM_HEADS, CHUNK), 0.02),
        "w_pool": nrm(ks[14], (DEPTH, len(POOL_WINDOWS), POOL_DIM, POOL_DIM), POOL_DIM ** -0.5),
        "pool_scale": 1.0 + nrm(ks[15], (DEPTH, POOL_WIDTH), 0.02),
        "ffn_w1": nrm(ks[16], (DEPTH, 2, D_MODEL, 2 * D_FF), D_MODEL ** -0.5),
        "ffn_w2": nrm(ks[17], (DEPTH, 2, D_FF, D_MODEL), D_FF ** -0.5),
    }


def reference(x_prompt, x_sample, cache_k, cache_v, c, c_ctx, w_mod, b_mod, norm_w, w_in, w_out,
              attn_sink, w_spatial, b_spatial, w_pool, pool_scale, ffn_w1, ffn_w2):
    y = x_prompt
    new_k, new_v = [], []
    for l in range(DEPTH):
        mod = adaln(c_ctx[None, :], w_mod[l], b_mod[l])
        y = ffn_half(y, mod, 0, norm_w[l, 0], norm_w[l, 1], ffn_w1[l, 0], ffn_w2[l, 0])
        h = modulate(rms_norm(y, norm_w[l, 2]), mod, 3)
        q, k, v, gu, gv, pl = project(h, w_in[l])
        attn = context_attention(q, k, v, attn_sink[l])
        out = merge_heads(attn, gu, gv, pl, w_spatial[l], b_spatial[l], w_pool[l], pool_scale[l], w_out[l])
        y = y + mod[:, 5][:, None, :] * rms_norm(out, norm_w[l, 3])
        y = ffn_half(y, mod, 6, norm_w[l, 4], norm_w[l, 5], ffn_w1[l, 1], ffn_w2[l, 1])
        new_k.append(k)
        new_v.append(v)
    y_prompt = y
    new_cache_k = jnp.stack(new_k, axis=1)
    new_cache_v = jnp.stack(new_v, axis=1)

    cos, sin = axial_rope_tables(x_sample.shape[1])
    z = x_sample
    for l in range(DEPTH):
        mod = adaln(c, w_mod[l], b_mod[l])
        z = ffn_half(z, mod, 0, norm_w[l, 0], norm_w[l, 1], ffn_w1[l, 0], ffn_w2[l, 0])
        h = modulate(rms_norm(z, norm_w[l, 2]), mod, 3)
        q, k, v, gu, gv, pl = project(h, w_in[l])
        q = apply_axial_rope(q, cos, sin)
        k = apply_axial_rope(k, cos, sin)
        attn = latent_attention(q, k, v, cache_k[:, l], cache_v[:, l], attn_sink[l])
        out = merge_heads(attn, gu, gv, pl, w_spatial[l], b_spatial[l], w_pool[l], pool_scale[l], w_out[l])
        z = z + mod[:, 5][:, None, :] * rms_norm(out, norm_w[l, 3])
        z = ffn_half(z, mod, 6, norm_w[l, 4], norm_w[l, 5], ffn_w1[l, 1], ffn_w2[l, 1])
    y_sample = z
    return (y_prompt, y_sample, new_cache_k, new_cache_v)
```

```python
import types
import numpy as np
from contextlib import ExitStack
import concourse.bass as bass
import concourse.mybir as mybir
from concourse.bass_utils import run_bass_kernel_spmd

F32 = mybir.dt.float32
BF16 = mybir.dt.bfloat16
AF = mybir.ActivationFunctionType
ALU = mybir.AluOpType
AX = mybir.AxisListType

NCORES = 8
L = 4
D = 1024
DFF = 2816
NKC = 8
NHC = 22
NMOD = 9
EPS = 1e-6
SCALE = 0.125
NEG = -30000.0
WIN_COLS = 2176
NPB = 8
NSB = 12
ENG = ('pe', 'act', 'dve', 'pool', 'sp')
POOL_WINDOWS = (2, 4, 8, 16)

CFG = dict(depth=4, prompt=True, sample=True, mixer=True, ffn2=True, ffn1=True, mix_stop=0)


class Buf:
    __slots__ = ('name', 'w', 'r', 'excl', 'wread')

    def __init__(self, name='', excl=False):
        self.name = name
        self.w = None
        self.r = {}
        self.wread = False
        self.excl = excl


def _freeze(fn):
    if fn.__closure__ is None:
        return fn
    cells = []
    for c in fn.__closure__:
        try:
            cells.append(types.CellType(c.cell_contents))
        except ValueError:
            cells.append(c)
    return types.FunctionType(fn.__code__, fn.__globals__, fn.__name__, fn.__defaults__, tuple(cells))


class Tracker:
    def __init__(self, nc, stack):
        self.nc = nc
        self.stack = stack
        self.ops = {e: [] for e in ENG}
        self.sem = {e: stack.enter_context(nc.semaphore('s_' + e)) for e in ENG}
        self.cnt = {e: 0 for e in ENG}
        self.waited = {e: {} for e in ENG}
        self.dsem = {}
        self.nops = 0

    def new_dsem(self, name):
        h = self.stack.enter_context(self.nc.semaphore('d_' + name))
        self.dsem[name] = [h, 0]
        return name

    def _h(self, key):
        return self.sem[key] if key in self.sem else self.dsem[key][0]

    def _wait(self, e, tok):
        if tok is None:
            return
        key, val = tok
        if e == 'pe' and key == 'pe':
            return
        if self.waited[e].get(key, 0) >= val:
            return
        self.waited[e][key] = val
        h = self._h(key)
        self.ops[e].append(lambda eng, h=h, val=val: eng.wait_ge(h, val))

    def _deps(self, e, reads, writes):
        for b in reads:
            self._wait(e, b.w)
        for b in writes:
            if not (b.w is not None and b.w[0] == e and not b.r and b not in reads):
                self._wait(e, b.w)
            for k, v in b.r.items():
                self._wait(e, (k, v))

    def _mark(self, tok, reads, writes):
        k, v = tok
        for b in reads:
            if b.r.get(k, 0) < v:
                b.r[k] = v
        for b in writes:
            b.w = tok
            b.r = {}

    def op(self, e, fn, reads=(), writes=(), inc=True):
        fn = _freeze(fn)
        promoted = [b for b in reads if b.excl]
        if promoted:
            reads = [b for b in reads if not b.excl]
            for b in promoted:
                if not (b.wread and b.w is not None and b.w[0] == e):
                    self._wait(e, b.w)
                for k, v in b.r.items():
                    self._wait(e, (k, v))
        self._deps(e, reads, writes)
        self.nops += 1
        if inc:
            self.cnt[e] += 1
            tok = (e, self.cnt[e])
            h = self.sem[e]
            self.ops[e].append(lambda eng, fn=fn, h=h: fn(eng).then_inc(h, 1))
        else:
            tok = (e, self.cnt[e] + 1)
            self.ops[e].append(lambda eng, fn=fn: fn(eng))
        self._mark(tok, reads, writes)
        for b in writes:
            b.wread = False
        for b in promoted:
            b.w = tok
            b.r = {}
            b.wread = True
        return tok

    def dma(self, q, out_ap, in_ap, dsem, reads=(), writes=()):
        for b in reads:
            self._wait(q, b.w)
        for b in writes:
            if not (b.w is not None and b.w[0] == dsem):
                self._wait(q, b.w)
            for k, v in b.r.items():
                self._wait(q, (k, v))
        self.dsem[dsem][1] += 16
        tok = (dsem, self.dsem[dsem][1])
        h = self.dsem[dsem][0]
        self.ops[q].append(lambda eng, o=out_ap, i=in_ap, h=h: eng.dma_start(out=o, in_=i).then_inc(h, 16))
        self._mark(tok, reads, writes)
        return tok

    def barrier(self):
        toks = [(e, self.cnt[e]) for e in ENG if self.cnt[e] > 0]
        toks += [(k, v[1]) for k, v in self.dsem.items() if v[1] > 0]
        for e in ENG:
            for t in toks:
                if t[0] != e or e in ('act', 'dve', 'pool'):
                    self._wait(e, t)


def split_tiles(blocks, maxb=4):
    n = len(blocks)
    if n == 0:
        return []
    nt = (n + maxb - 1) // maxb
    base, rem = divmod(n, nt)
    out, i = [], 0
    for t in range(nt):
        k = base + (1 if t < rem else 0)
        out.append(blocks[i:i + k])
        i += k
    return out


def make_groups(tiles, maxblocks):
    groups, cur, nb = [], [], 0
    for t in tiles:
        if cur and nb + len(t) > maxblocks:
            groups.append(cur)
            cur, nb = [], 0
        cur.append(t)
        nb += len(t)
    if cur:
        groups.append(cur)
    return groups


class Builder:
    def __init__(self, cfg):
        self.cfg = cfg
        self.uid = 0

    def sb(self, scope, name, shape, dt):
        self.uid += 1
        return scope.enter_context(self.nc.sbuf_tensor('%s_%d' % (name, self.uid), list(shape), dt))

    def build(self):
        cfg = self.cfg
        nc = bass.Bass("TRN2", target_bir_lowering=False)
        self.nc = nc
        dram = {}

        def din(name, shape):
            dram[name] = nc.dram_tensor(name, list(shape), F32, kind="ExternalInput").ap()

        def dout(name, shape):
            dram[name] = nc.dram_tensor(name, list(shape), F32, kind="ExternalOutput").ap()

        din('xp', [128, NKC, NPB * 128])
        din('xs', [128, NKC, NSB * 128])
        din('cT', [128, NKC, 2])
        din('rope', [128, 2, NSB * 128])
        din('masko', [128, 2, 512])
        din('PMo', [128, 16, 128])
        din('ckT', [L, 64, 2, 512])
        din('cv', [L, 128, 4, 2, 128])
        din('w_mod', [L, D, NMOD * D])
        din('w1', [L, 2, D, 2 * DFF])
        din('w2', [L, 2, DFF, D])
        din('w_in', [L, D, WIN_COLS])
        din('w_out', [L, D, D])
        din('nw', [128, L, 6, NKC])
        din('bmod', [128, L, 72])
        din('wsT', [L, 128, 4, 128])
        din('bbc', [L, 64, 4, 128])
        din('sinkb', [64, L, 8])
        din('pscale', [64, L, 4])
        din('wpool', [64, L, 4, 64])
        din('cmat', [128, 320])
        din('masks', [128, 2, 512])
        din('PM', [128, 20, 128])
        dout('yp', [128, NKC, NPB * 128])
        dout('ys', [128, NKC, 512])
        dout('nkT', [L, 128, NPB * 128])
        dout('nv', [L, NPB * 128, 128])
        self.dram = dram

        with ExitStack() as stack:
            T = Tracker(nc, stack)
            self.T = T
            for n in ('const', 'constf', 'mc', 'mcf', 'x', 'x1', 'x2', 'y', 'w1a', 'w1b', 'w2a', 'w2b', 'w2c', 'wia', 'wib', 'wtm',
                      'woa', 'wob', 'wma', 'wmb', 'wmc', 'wmd', 'ko', 'vo', 'cache'):
                T.new_dsem(n)
            self.ps = []
            self.psb = []
            for i in range(8):
                self.ps.append(stack.enter_context(nc.psum_tensor('ps%d' % i, [128, 512], F32)))
                self.psb.append(Buf('ps%d' % i, excl=True))
            self.rot = {}
            P = stack
            self.x = self.sb(P, 'x', [128, NKC, NSB * 128], F32)
            self.xcb = [[Buf('x%d_%d' % (b, kc)) for kc in range(NKC)] for b in range(NSB)]
            self.coefA = self.sb(P, 'coefA', [128, L, 3, NKC, 2], F32)
            self.coefB = self.sb(P, 'coefB', [128, L, 3, NKC, 2], F32)
            self.coefG = self.sb(P, 'coefG', [128, L, 3, NKC, 2], F32)
            self.coef_b = Buf('coef')
            self.cm = self.sb(P, 'cm', [128, 320], BF16)
            self.esink = self.sb(P, 'esink', [64, L, 8], F32)
            self.pscale = self.sb(P, 'pscale', [64, L, 4], F32)
            self.wpool = self.sb(P, 'wpool', [64, L, 4, 64], BF16)
            self.const_b = Buf('const')
            self.constf_b = Buf('constf')
            self.epsc = self.sb(P, 'epsc', [128, 1], F32)
            self.ident = self.cm[:, 0:128]
            self.meanm = self.cm[:, 128:256]
            self.ones64 = self.cm[:, 256:320]

            if cfg['prompt']:
                T.dma('sp', self.x[:, :, 0:NPB * 128], dram['xp'], 'x',
                      writes=[c_ for b in range(NPB) for c_ in self.xcb[b]])
            self.prologue()
            depth = cfg['depth']
            if cfg['prompt']:
                self.run_supergroup('p', depth)
            if cfg['sample']:
                self.run_supergroup('s', depth)
            T.barrier()

            with nc.Block() as block:
                @block.tensor
                def _(e):
                    for f in T.ops['pe']:
                        f(e)

                @block.scalar
                def _(e):
                    for f in T.ops['act']:
                        f(e)

                @block.vector
                def _(e):
                    for f in T.ops['dve']:
                        f(e)

                @block.gpsimd
                def _(e):
                    for f in T.ops['pool']:
                        f(e)

                @block.sync
                def _(e):
                    for f in T.ops['sp']:
                        f(e)
        return nc

    def bank(self, pool):
        i = self.rot.get(pool, 0)
        self.rot[pool] = i + 1
        b = pool[i % len(pool)]
        return self.ps[b], self.psb[b]

    def prologue(self):
        T, nc, dram = self.T, self.nc, self.dram
        x = self.x
        depth = self.cfg['depth']
        T.dma('pool', self.cm[:, :], dram['cmat'], 'const', writes=[self.const_b])
        T.dma('pool', self.wpool[:, :, :, :], dram['wpool'], 'const', writes=[self.const_b])
        T.dma('sp', self.esink[:, :, :], dram['sinkb'], 'constf', writes=[self.constf_b])
        T.dma('sp', self.pscale[:, :, :], dram['pscale'], 'constf', writes=[self.constf_b])
        modT = [x[:, l, 1280:1424].rearrange("p (j c) -> p j c", c=2) for l in range(L)]
        nwv = [x[:, l, 1424:1472].rearrange("p (i k) -> p i k", k=NKC) for l in range(L)]
        bmv = [x[:, 4 + l, 1280:1352] for l in range(L)]
        cTv = x[:, 4, 1360:1376]
        scv = x[:, 5, 1360:1368].bitcast(BF16).rearrange("p (k c) -> p k c", c=2)
        NWB = 4
        wmv = [x[:, :, 1024 + 64 * i:1024 + 64 * (i + 1)].bitcast(BF16) for i in range(NWB)]
        T.dma('sp', cTv, dram['cT'].rearrange("p k c -> p (k c)"), 'constf', writes=[self.constf_b])
        T.dma('sp', x[:, 0:L, 1424:1472], dram['nw'].rearrange("p l i k -> p l (i k)"), 'constf',
              writes=[self.constf_b])
        T.dma('sp', x[:, 4:4 + L, 1280:1352], dram['bmod'], 'constf', writes=[self.constf_b])
        self.const_b.w = ('const', T.dsem['const'][1])
        self.constf_b.w = ('constf', T.dsem['constf'][1])
        cb = self.constf_b
        T.op('dve', lambda e: e.memset(self.epsc[:, :], EPS), writes=[cb])
        T.op('act', lambda e: e.activation(out=self.esink[:, :, :], in_=self.esink[:, :, :], func=AF.Exp),
             reads=[cb], writes=[cb])
        scb = Buf('sc')
        T.op('act', lambda e: e.activation(out=scv, in_=cTv.rearrange("p (k c) -> p k c", c=2), func=AF.Silu),
             reads=[cb], writes=[scb])
        modb = [Buf('modT%d' % l) for l in range(L)]
        wmb = [Buf('wm%d' % i) for i in range(NWB)]
        wsem = ['wma', 'wmb', 'wmc', 'wmd']
        NST = 72
        seq = [(l, st) for l in range(depth) for st in range(NST)]

        def load(i):
            l, st = seq[i]
            src = dram['w_mod'][l].rearrange("(kc p) n -> p kc n", p=128)[:, :, 128 * st:128 * st + 128]
            T.dma('pool', wmv[i % NWB], src, wsem[i % NWB], writes=[wmb[i % NWB]])

        def step(i):
            l, st = seq[i]
            pst, psb = self.bank(self.bg_pool)
            for kc in range(NKC):
                T.op('pe', lambda e: e.matmul(
                    pst[:, 0:2], lhsT=wmv[i % NWB][:, kc, :],
                    rhs=scv[:, kc, :], start=(kc == 0), stop=(kc == NKC - 1)),
                    reads=[wmb[i % NWB], scb], writes=[psb], inc=(kc == NKC - 1))
            if i + NWB < len(seq):
                load(i + NWB)
            T.op('dve', lambda e: e.tensor_tensor(
                out=modT[l][:, st, :], in0=pst[:, 0:2], in1=bmv[l][:, st:st + 1].to_broadcast([128, 2]),
                op=ALU.add), reads=[psb, cb], writes=[modb[l]])
            if st == NST - 1:
                for s_ in range(3):
                    sh, scl, gt = 3 * s_, 3 * s_ + 1, 3 * s_ + 2
                    npre, npost = 2 * s_, 2 * s_ + 1
                    gmul = 1.0 if s_ == 1 else 0.5
                    A = self.coefA[:, l, s_, :, :]
                    Bc = self.coefB[:, l, s_, :, :]
                    G = self.coefG[:, l, s_, :, :]
                    T.op('dve', lambda e: e.scalar_tensor_tensor(
                        out=A, in0=modT[l][:, scl * 8:(scl + 1) * 8, :], scalar=1.0,
                        in1=nwv[l][:, npre, :].unsqueeze(2).to_broadcast([128, NKC, 2]),
                        op0=ALU.add, op1=ALU.mult), reads=[modb[l], cb], writes=[self.coef_b])
                    T.op('dve', lambda e: e.tensor_copy(out=Bc, in_=modT[l][:, sh * 8:(sh + 1) * 8, :]),
                         reads=[modb[l]], writes=[self.coef_b])
                    T.op('dve', lambda e: e.scalar_tensor_tensor(
                        out=G, in0=modT[l][:, gt * 8:(gt + 1) * 8, :], scalar=gmul,
                        in1=nwv[l][:, npost, :].unsqueeze(2).to_broadcast([128, NKC, 2]),
                        op0=ALU.mult, op1=ALU.mult), reads=[modb[l], cb], writes=[self.coef_b])

        self.bg_pool = (6, 7)
        for i in range(min(NWB, len(seq))):
            load(i)
        for i in range(min(NST, len(seq))):
            step(i)
        self.bg = [(lambda i=i: step(i)) for i in range(NST, len(seq))]
        T.barrier()

    def bg_step(self, n):
        for _ in range(n):
            if self.bg:
                self.bg.pop(0)()

    def prenorm(self, tile_cols, n, xbufs, h, hoff, hbuf, l, s, cv, scr):
        T = self.T
        sq, sqb, rstd, rstdb, tmp, tmpb = scr
        x = self.x
        xo = tile_cols
        for kc in range(NKC):
            if kc % 2 == 0:
                T.op('act', lambda e, kc=kc: e.activation(out=sq[:, kc, 0:n], in_=x[:, kc, xo:xo + n], func=AF.Square),
                     reads=xbufs, writes=[sqb[kc]])
            else:
                T.op('dve', lambda e, kc=kc: e.tensor_tensor(out=sq[:, kc, 0:n], in0=x[:, kc, xo:xo + n],
                                                             in1=x[:, kc, xo:xo + n], op=ALU.mult),
                     reads=xbufs, writes=[sqb[kc]])
        pst, psb = self.bank((6, 7))
        for kc in range(NKC):
            T.op('pe', lambda e, kc=kc, pst=pst: e.matmul(pst[:, 0:n], lhsT=self.meanm, rhs=sq[:, kc, 0:n],
                                                          start=(kc == 0), stop=(kc == NKC - 1)),
                 reads=[sqb[kc], self.const_b], writes=[psb], inc=(kc == NKC - 1))
        T.op('act', lambda e, pst=pst: e.activation(out=rstd[:, 0:n], in_=pst[:, 0:n], func=AF.Ln,
                                                    bias=self.epsc[:, 0:1], scale=1.0), reads=[psb, self.constf_b], writes=[rstdb])
        T.op('act', lambda e: e.activation(out=rstd[:, 0:n], in_=rstd[:, 0:n], func=AF.Exp, scale=-0.5),
                     reads=[rstdb], writes=[rstdb])
        for kc in range(NKC):
            tb, tbb = tmp[kc % 2], tmpb[kc % 2]
            T.op('dve', lambda e, kc=kc, tb=tb: e.tensor_tensor(
                out=tb[:, 0:n], in0=x[:, kc, xo:xo + n], in1=rstd[:, 0:n], op=ALU.mult),
                reads=list(xbufs) + [rstdb], writes=[tbb])
            T.op('act', lambda e, kc=kc, tb=tb: e.activation(
                out=h[:, kc, hoff:hoff + n], in_=tb[:, 0:n], func=AF.Identity,
                bias=self.coefB[:, l, s, kc, cv:cv + 1], scale=self.coefA[:, l, s, kc, cv:cv + 1]),
                reads=[tbb, self.coef_b], writes=[hbuf])

    def postnorm(self, f, foff, fbufs, n, xo, xbufs, l, s, cv, scr):
        T = self.T
        sq, sqb, rstd, rstdb, tmp, tmpb = scr
        x = self.x
        for m in range(NKC):
            T.op('act', lambda e, m=m: e.activation(out=sq[:, m, 0:n], in_=f[:, m, foff:foff + n], func=AF.Square),
                 reads=[fbufs[m]], writes=[sqb[m]])
        pst, psb = self.bank((6, 7))
        for m in range(NKC):
            T.op('pe', lambda e, m=m, pst=pst: e.matmul(pst[:, 0:n], lhsT=self.meanm, rhs=sq[:, m, 0:n],
                                                        start=(m == 0), stop=(m == NKC - 1)),
                 reads=[sqb[m], self.const_b], writes=[psb], inc=(m == NKC - 1))
        T.op('act', lambda e, pst=pst: e.activation(out=rstd[:, 0:n], in_=pst[:, 0:n], func=AF.Ln,
                                                    bias=self.epsc[:, 0:1], scale=1.0), reads=[psb, self.constf_b], writes=[rstdb])
        T.op('act', lambda e: e.activation(out=rstd[:, 0:n], in_=rstd[:, 0:n], func=AF.Exp, scale=-0.5),
                     reads=[rstdb], writes=[rstdb])
        for m in range(NKC):
            tb, tbb = tmp[m % 2], tmpb[m % 2]
            T.op('dve', lambda e, m=m, tb=tb: e.scalar_tensor_tensor(
                out=tb[:, 0:n], in0=f[:, m, foff:foff + n], scalar=self.coefG[:, l, s, m, cv:cv + 1],
                in1=rstd[:, 0:n], op0=ALU.mult, op1=ALU.mult), reads=[fbufs[m], rstdb, self.coef_b], writes=[tbb])
            T.op('dve', lambda e, m=m, tb=tb: e.tensor_tensor(
                out=x[:, m, xo:xo + n], in0=x[:, m, xo:xo + n], in1=tb[:, 0:n], op=ALU.add),
                reads=[tbb], writes=xbufs)

    def norm_scratch(self, S):
        sq = self.sb(S, 'sq', [128, NKC, 512], BF16)
        sqb = [Buf('sq%d' % i) for i in range(NKC)]
        rstd = self.sb(S, 'rstd', [128, 512], F32)
        tmp = [self.sb(S, 'tmp%d' % i, [128, 512], F32) for i in range(2)]
        return (sq, sqb, rstd, Buf('rstd'), tmp, [Buf('tmp0'), Buf('tmp1')])

    def ffn(self, group, l, fi, s, cv):
        T, nc, dram = self.T, self.nc, self.dram
        x = self.x
        ntl = len(group)
        NT = sum(len(t) for t in group) * 128
        with ExitStack() as S:
            h = self.sb(S, 'h', [128, NKC, NT], BF16)
            hid = self.sb(S, 'hid', [128, NHC, NT], BF16)
            f = self.sb(S, 'f', [128, NKC, NT], F32)
            sq = self.sb(S, 'sq', [128, NKC, 512], BF16)
            sqb = [Buf('sq%d' % i) for i in range(NKC)]
            rstd = self.sb(S, 'rstd', [128, ntl, 512], F32)
            rstdb = [Buf('rstd%d' % i) for i in range(ntl)]
            tmp = [self.sb(S, 'tmp%d' % i, [128, 512], F32) for i in range(4)]
            tmpb = [Buf('tmp%d' % i) for i in range(4)]
            sg = [self.sb(S, 'sg%d' % i, [128, 512], F32) for i in range(2)]
            sgb = [Buf('sg0'), Buf('sg1')]
            sqf = [self.sb(S, 'sqf%d' % i, [128, 512], BF16) for i in range(4)]
            sqfb = [Buf('sqf%d' % i) for i in range(4)]
            w1t = [self.sb(S, 'w1t%d' % i, [128, NKC, 2, 256], BF16) for i in range(2)]
            w1b = [Buf('w1t0'), Buf('w1t1')]
            w2t = [self.sb(S, 'w2t%d' % i, [128, NHC, 128], BF16) for i in range(3)]
            w2b = [Buf('w2t%d' % i) for i in range(3)]
            w1s, w2s = ['w1a', 'w1b'], ['w2a', 'w2b', 'w2c']
            w1src = dram['w1'][l, fi].rearrange("(kc p) n -> p kc n", p=128)
            w2src = dram['w2'][l, fi].rearrange("(kc p) n -> p kc n", p=128)

            def load1(sp):
                i = sp % 2
                T.dma('pool', w1t[i][:, :, 0, :], w1src[:, :, 256 * sp:256 * sp + 256], w1s[i], writes=[w1b[i]])
                T.dma('pool', w1t[i][:, :, 1, :], w1src[:, :, DFF + 256 * sp:DFF + 256 * sp + 256], w1s[i],
                      writes=[w1b[i]])

            def load2(m):
                i = m % 3
                T.dma('pool', w2t[i][:, :, :], w2src[:, :, 128 * m:128 * m + 128], w2s[i], writes=[w2b[i]])

            load1(0)
            load1(1)
            scr = (sq, sqb, rstd[:, 0, :], rstdb[0], [tmp[0], tmp[1]], [tmpb[0], tmpb[1]])
            tiles = []
            ho = 0
            for ti, t in enumerate(group):
                n = len(t) * 128
                xo = t[0] * 128
                xbufs = [c_ for b in t for c_ in self.xcb[b]]
                hbuf = Buf('h')
                tiles.append((xo, n, ho, ho, xbufs, hbuf))
                ho += n

            def do_pre(ti):
                xo, n, ho, hc, xbufs, hbuf = tiles[ti]
                self.prenorm(xo, n, xbufs, h, ho, hbuf, l, s, cv, scr)

            do_pre(0)
            load2(0)
            load2(1)
            load2(2)
            hidb = [[Buf('hid') for _ in tiles] for _ in range(NHC)]
            gu_pool = (0, 1, 2, 3)
            k = 0
            order = []
            for ti in range(ntl):
                order += [(0, ti), (1, ti)]
            for sp in range(2, NHC // 2):
                order += [(sp, ti) for ti in range(ntl)]
            done_cnt = {}
            for (sp, ti) in order:
                wi = sp % 2
                if True:
                    xo, n, ho, hc, xbufs, hbuf = tiles[ti]
                    for jj in range(2):
                        j = 2 * sp + jj
                        psg, psgb = self.bank(gu_pool)
                        psu, psub = self.bank(gu_pool)
                        for gu, (pt, pb) in enumerate(((psg, psgb), (psu, psub))):
                            for kc in range(NKC):
                                T.op('pe', lambda e: e.matmul(
                                    pt[:, 0:n], lhsT=w1t[wi][:, kc, gu, jj * 128:(jj + 1) * 128],
                                    rhs=h[:, kc, ho:ho + n], start=(kc == 0), stop=(kc == NKC - 1)),
                                    reads=[w1b[wi], hbuf], writes=[pb], inc=(kc == NKC - 1))
                        si = k % 2
                        k += 1
                        T.op('act', lambda e: e.activation(out=sg[si][:, 0:n], in_=psg[:, 0:n], func=AF.Silu),
                             reads=[psgb], writes=[sgb[si]])
                        T.op('dve', lambda e: e.tensor_tensor(
                            out=hid[:, j, ho:ho + n], in0=sg[si][:, 0:n], in1=psu[:, 0:n], op=ALU.mult),
                            reads=[sgb[si], psub], writes=[hidb[j][ti]])
                if sp == 0 and ti + 1 < ntl:
                    do_pre(ti + 1)
                done_cnt[sp] = done_cnt.get(sp, 0) + 1
                if done_cnt[sp] == ntl:
                    self.bg_step(3)
                if done_cnt[sp] == ntl and sp + 2 < NHC // 2:
                    load1(sp + 2)
            fb = [[Buf('f') for _ in range(NKC)] for _ in tiles]
            nbanks = (6, 7)
            psn = [(self.ps[nbanks[ti]], self.psb[nbanks[ti]]) for ti in range(ntl)]
            pend = []
            q = 0

            def flush(keep):
                while len(pend) > keep:
                    ti_, m_, qi_, n_ = pend.pop(0)
                    pt_, pb_ = psn[ti_]
                    T.op('pe', lambda e: e.matmul(pt_[:, 0:n_], lhsT=self.meanm, rhs=sqf[qi_][:, 0:n_],
                                                  start=(m_ == 0), stop=(m_ == NKC - 1)),
                         reads=[sqfb[qi_], self.const_b], writes=[pb_], inc=True)

            order2 = [(m, ti) for m in range(NKC - 3) for ti in range(ntl)]
            for ti in range(ntl):
                order2 += [(m, ti) for m in range(NKC - 3, NKC)]
            done2 = {}

            def post(ti):
                xo, n, ho, hc, xbufs, hbuf = tiles[ti]
                pt_, pb_ = psn[ti]
                T.op('act', lambda e: e.activation(out=rstd[:, ti, 0:n], in_=pt_[:, 0:n], func=AF.Ln,
                                                   bias=self.epsc[:, 0:1], scale=1.0),
                     reads=[pb_, self.constf_b], writes=[rstdb[ti]])
                T.op('act', lambda e: e.activation(out=rstd[:, ti, 0:n], in_=rstd[:, ti, 0:n], func=AF.Exp, scale=-0.5),
                     reads=[rstdb[ti]], writes=[rstdb[ti]])
                for m in range(NKC + 1):
                    if m < NKC:
                        tb, tbb = tmp[m % 4], tmpb[m % 4]
                        T.op('dve', lambda e: e.tensor_tensor(out=tb[:, 0:n], in0=f[:, m, ho:ho + n],
                                                              in1=rstd[:, ti, 0:n], op=ALU.mult),
                             reads=[fb[ti][m], rstdb[ti]], writes=[tbb])
                    if m >= 1:
                        m1 = m - 1
                        tb1, tbb1 = tmp[m1 % 4], tmpb[m1 % 4]
                        T.op('dve', lambda e: e.tensor_tensor(out=x[:, m1, xo:xo + n], in0=x[:, m1, xo:xo + n],
                                                              in1=tb1[:, 0:n], op=ALU.add),
                             reads=[tbb1], writes=[self.xcb[b_][m1] for b_ in range(xo // 128, (xo + n) // 128)])

            for (m, ti) in order2:
                wi = m % 3
                xo, n, ho, hc, xbufs, hbuf = tiles[ti]
                psf, psfb = self.bank((0, 1, 2, 3, 4, 5))
                for kc in range(NHC):
                    T.op('pe', lambda e: e.matmul(
                        psf[:, 0:n], lhsT=w2t[wi][:, kc, :], rhs=hid[:, kc, ho:ho + n],
                        start=(kc == 0), stop=(kc == NHC - 1)),
                        reads=[w2b[wi], hidb[kc][ti]], writes=[psfb], inc=(kc == NHC - 1))
                flush(1)
                qi = q % 4
                q += 1
                T.op('act', lambda e: e.activation(out=sqf[qi][:, 0:n], in_=psf[:, 0:n], func=AF.Square),
                     reads=[psfb], writes=[sqfb[qi]])
                T.op('dve', lambda e: e.tensor_scalar(
                    out=f[:, m, ho:ho + n], in0=psf[:, 0:n], scalar1=self.coefG[:, l, s, m, cv:cv + 1],
                    scalar2=None, op0=ALU.mult), reads=[psfb, self.coef_b], writes=[fb[ti][m]])
                pend.append((ti, m, qi, n))
                done2[m] = done2.get(m, 0) + 1
                if done2[m] == ntl and m + 3 < NKC:
                    load2(m + 3)
                if done2[m] == ntl:
                    self.bg_pool = (0, 1, 2, 3, 4, 5)
                    self.bg_step(1)
                    self.bg_pool = (6, 7)
                if m == NKC - 1:
                    flush(0)
                    post(ti)
            T.barrier()

    def run_supergroup(self, kind, depth):
        T, dram = self.T, self.dram
        cfg = self.cfg
        x = self.x
        if kind == 's':
            self.bg_step(10 ** 6)
            T.barrier()
        if kind == 'p':
            nb = NPB
            cv = 0
        else:
            nb = NSB
            cv = 1
            for (b0, b1, sem_) in ((0, 4, 'x'), (4, 8, 'x1'), (8, 12, 'x2')):
                T.dma('sp', x[:, :, 128 * b0:128 * b1], dram['xs'][:, :, 128 * b0:128 * b1], sem_,
                      writes=[c_ for b in range(b0, b1) for c_ in self.xcb[b]])
        for l in range(depth):
            if kind == 'p':
                kvb = list(range(NPB))
                fullb = list(range(NPB))
                gmax = 8
            else:
                kvb = list(range(l, NSB - l))
                fullb = list(range(l + 1, NSB - 1 - l))
                gmax = 8
            if cfg['ffn1']:
                for g in make_groups(split_tiles(kvb, 4), gmax):
                    self.ffn(g, l, 0, 0, cv)
            if cfg['mixer']:
                self.mixer(kind, l, kvb, fullb, cv)
            if cfg['ffn2']:
                for g in make_groups(split_tiles(fullb, 4), gmax):
                    self.ffn(g, l, 1, 2, cv)
        if kind == 'p':
            self.bg_step(10 ** 6)
            T.dma('sp', dram['yp'], x[:, :, 0:NPB * 128], 'y', reads=[c_ for b in range(NPB) for c_ in self.xcb[b]])
        else:
            T.dma('sp', dram['ys'], x[:, :, 512:1024], 'y', reads=[c_ for b in range(4, 8) for c_ in self.xcb[b]])
        T.barrier()

    def mixer(self, kind, l, kvb, fullb, cv):
        T, nc, dram = self.T, self.nc, self.dram
        x = self.x
        is_s = (kind == 's')
        NB = NSB if is_s else NPB
        NTs = NB * 128
        with ExitStack() as SA:
            qT = self.sb(SA, 'qT', [64, 8, NTs], BF16)
            kT = self.sb(SA, 'kT', [64, 2, NTs], BF16)
            guT = self.sb(SA, 'guT', [128, 2, NTs], BF16)
            vtm = self.sb(SA, 'vtm', [128, NB, 2, 128], BF16)
            vh = self.sb(SA, 'vh', [128, NB, 256], BF16)
            pltm = self.sb(SA, 'pltm', [128, NB, 256], BF16)
            qb = [Buf('q%d' % b) for b in range(NB)]
            kb = [Buf('k%d' % b) for b in range(NB)]
            gub = [Buf('gu%d' % b) for b in range(NB)]
            vb = [Buf('v%d' % b) for b in range(NB)]
            vhb = [Buf('vh%d' % b) for b in range(NB)]
            plb = [Buf('pl%d' % b) for b in range(NB)]
            T.op('dve', lambda e: e.memset(vtm[:, :, :, 64:128], 1.0), writes=vb)
            with ExitStack() as S:
                scrs = [self.norm_scratch(S) for _ in range(2)]
                h8s = [self.sb(S, 'h8_%d' % i, [128, NKC, 512], BF16) for i in range(2)]
                wst = [self.sb(S, 'wst%d' % i, [128, NKC, 256], BF16) for i in range(2)]
                wstb = [Buf('wst0'), Buf('wst1')]
                wss = ['wia', 'wib']
                wtm = self.sb(S, 'wtm', [128, NKC, 640], BF16)
                wtmb = Buf('wtm')
                mcfb = Buf('mcf')
                if is_s:
                    rope = self.sb(S, 'rope', [128, 2, NTs], F32)
                    T.dma('sp', rope[:, :, :], dram['rope'], 'mcf', writes=[mcfb])
                    rt = [self.sb(S, 'rt%d' % i, [128, 512], F32) for i in range(4)]
                    rtb = [Buf('rt%d' % i) for i in range(4)]
                else:
                    kst = [self.sb(S, 'kst%d' % i, [128, 512], F32) for i in range(2)]
                    kstb = [Buf('kst0'), Buf('kst1')]
                    vst = [self.sb(S, 'vst%d' % i, [128, 4, 128], F32) for i in range(2)]
                    vstb = [Buf('vst0'), Buf('vst1')]
                sqg = [self.sb(S, 'sqg%d' % i, [128, 256], F32) for i in range(2)]
                sqgb = [Buf('sqg0'), Buf('sqg1')]
                ss = [self.sb(S, 'ss%d' % i, [128, 4], F32) for i in range(2)]
                ssb = [Buf('ss0'), Buf('ss1')]
                wsrc = dram['w_in'][l].rearrange("(kc p) n -> p kc n", p=128)
                stripes = list(range(6))
                tiles = split_tiles(kvb, 4)
                seq = [(ti, st) for ti in range(len(tiles)) for st in stripes]

                def loadst(i):
                    ti, st = seq[i]
                    T.dma('pool', wst[i % 2][:, :, :], wsrc[:, :, 256 * st:256 * st + 256], wss[i % 2],
                          writes=[wstb[i % 2]])

                loadst(0)
                loadst(1)
                gp = (0, 1, 2, 3, 4, 5)
                rk = 0
                hbufs = {}

                def do_prenorm(ti):
                    t = tiles[ti]
                    hb_ = Buf('h8')
                    hbufs[ti] = hb_
                    self.prenorm(t[0] * 128, len(t) * 128, [c_ for b in t for c_ in self.xcb[b]], h8s[ti % 2], 0, hb_, l, 1, cv,
                                 scrs[ti % 2])

                do_prenorm(0)
                for ti, t in enumerate(tiles):
                    n = len(t) * 128
                    xo = t[0] * 128
                    h8 = h8s[ti % 2]
                    hbuf = hbufs[ti]
                    T.dma('pool', wtm[:, :, :], wsrc[:, :, 1536:2176], 'wtm', writes=[wtmb])
                    if ti + 1 < len(tiles):
                        do_prenorm(ti + 1)
                    for st in stripes:
                        i = ti * len(stripes) + st
                        wi = i % 2

                        def grp(g):
                            pt, pb = self.bank(gp)
                            for kc in range(NKC):
                                T.op('pe', lambda e: e.matmul(
                                    pt[:, 0:n], lhsT=wst[wi][:, kc, 128 * g:128 * g + 128], rhs=h8[:, kc, 0:n],
                                    start=(kc == 0), stop=(kc == NKC - 1)),
                                    reads=[wstb[wi], hbuf], writes=[pb], inc=(kc == NKC - 1))
                            return pt, pb

                        if st <= 4:
                            if st < 4:
                                heads = (2 * st, 2 * st + 1)
                                dst, dbufs = qT, qb
                            else:
                                heads = (0, 1)
                                dst, dbufs = kT, kb
                            wb = [dbufs[b] for b in t]
                            pt, pb = grp(0)
                            if is_s:
                                pp, ppb = grp(1)
                                r0, r0b = rt[rk % 4], rtb[rk % 4]
                                r1, r1b = rt[(rk + 1) % 4], rtb[(rk + 1) % 4]
                                rk += 2
                                T.op('dve', lambda e: e.tensor_tensor(
                                    out=r0[:, 0:n], in0=pt[:, 0:n], in1=rope[:, 0, xo:xo + n], op=ALU.mult),
                                    reads=[pb, mcfb], writes=[r0b])
                                T.op('dve', lambda e: e.tensor_tensor(
                                    out=r1[:, 0:n], in0=pp[:, 0:n], in1=rope[:, 1, xo:xo + n], op=ALU.mult),
                                    reads=[ppb, mcfb], writes=[r1b])
                                for hi, head in enumerate(heads):
                                    T.op('dve', lambda e: e.tensor_tensor(
                                        out=dst[:, head, xo:xo + n], in0=r0[64 * hi:64 * hi + 64, 0:n],
                                        in1=r1[64 * hi:64 * hi + 64, 0:n], op=ALU.add),
                                        reads=[r0b, r1b], writes=wb)
                            else:
                                for hi, head in enumerate(heads):
                                    T.op('act', lambda e: e.activation(
                                        out=dst[:, head, xo:xo + n], in_=pt[64 * hi:64 * hi + 64, 0:n], func=AF.Copy),
                                        reads=[pb], writes=wb)
                                if st == 4:
                                    ks, ksb = kst[ti % 2], kstb[ti % 2]
                                    T.op('dve', lambda e: e.tensor_copy(out=ks[:, 0:n], in_=pt[:, 0:n]),
                                         reads=[pb], writes=[ksb])
                                    T.dma('sp', dram['nkT'][l][:, xo:xo + n], ks[:, 0:n], 'ko', reads=[ksb])
                        else:
                            for c in range(2):
                                pt, pb = grp(c)
                                T.op('act', lambda e: e.activation(
                                    out=guT[:, c, xo:xo + n], in_=pt[:, 0:n], func=AF.Copy),
                                    reads=[pb], writes=[gub[b] for b in t])
                        if i + 2 < len(seq):
                            loadst(i + 2)
                    for bi, b in enumerate(t):
                        pa, pab = self.bank(gp)
                        pbk, pbb = self.bank(gp)
                        for kc in range(NKC):
                            T.op('pe', lambda e: e.matmul(
                                pa[:, 0:384], lhsT=h8[:, kc, 128 * bi:128 * bi + 128], rhs=wtm[:, kc, 0:384],
                                start=(kc == 0), stop=(kc == NKC - 1)),
                                reads=[wtmb, hbuf], writes=[pab], inc=(kc == NKC - 1))
                        for kc in range(NKC):
                            T.op('pe', lambda e: e.matmul(
                                pbk[:, 0:256], lhsT=h8[:, kc, 128 * bi:128 * bi + 128], rhs=wtm[:, kc, 384:640],
                                start=(kc == 0), stop=(kc == NKC - 1)),
                                reads=[wtmb, hbuf], writes=[pbb], inc=(kc == NKC - 1))
                        T.op('act', lambda e: e.activation(
                            out=vtm[:, b, :, 0:64], in_=pa[:, 0:128].rearrange("p (h d) -> p h d", d=64), func=AF.Copy),
                            reads=[pab], writes=[vb[b]])
                        if not is_s:
                            vs, vsb = vst[ti % 2], vstb[ti % 2]
                            T.op('dve', lambda e: e.tensor_copy(out=vs[:, bi, :], in_=pa[:, 0:128]),
                                 reads=[pab], writes=[vsb])
                        si = bi % 2
                        T.op('act', lambda e: e.activation(out=sqg[si][:, :], in_=pa[:, 128:384], func=AF.Square),
                             reads=[pab], writes=[sqgb[si]])
                        T.op('dve', lambda e: e.tensor_reduce(
                            out=ss[si][:, :], in_=sqg[si][:, :].rearrange("p (h d) -> p h d", d=64),
                            axis=AX.X, op=ALU.add), reads=[sqgb[si]], writes=[ssb[si]])
                        T.op('act', lambda e: e.activation(
                            out=ss[si][:, :], in_=ss[si][:, :], func=AF.Ln, bias=self.epsc[:, 0:1],
                            scale=1.0 / 64.0), reads=[ssb[si], self.constf_b], writes=[ssb[si]])
                        T.op('act', lambda e: e.activation(out=ss[si][:, :], in_=ss[si][:, :], func=AF.Exp, scale=-0.5),
                     reads=[ssb[si]], writes=[ssb[si]])
                        T.op('dve', lambda e: e.tensor_tensor(
                            out=vh[:, b, :].rearrange("p (h d) -> p h d", d=64),
                            in0=pa[:, 128:384].rearrange("p (h d) -> p h d", d=64),
                            in1=ss[si][:, :].unsqueeze(2).to_broadcast([128, 4, 64]), op=ALU.mult),
                            reads=[pab, ssb[si]], writes=[vhb[b]])
                        T.op('act', lambda e: e.activation(out=pltm[:, b, :], in_=pbk[:, 0:256], func=AF.Copy),
                             reads=[pbb], writes=[plb[b]])
                    if not is_s:
                        T.dma('sp', dram['nv'][l][xo:xo + n, :].rearrange("(b p) c -> p b c", p=128),
                              vs[:, 0:len(t), :], 'vo', reads=[vsb])
                T.barrier()
            with ExitStack() as S:
                sqf = [self.sb(S, 'sqfm%d' % i, [128, 512], BF16) for i in range(4)]
                sqfb = [Buf('sqfm%d' % i) for i in range(4)]
                rstdm = self.sb(S, 'rstdm', [128, 512], F32)
                rstdmb = Buf('rstdm')
                tmpm = [self.sb(S, 'tmpm%d' % i, [128, 512], F32) for i in range(4)]
                tmpmb = [Buf('tmpm%d' % i) for i in range(4)]
                qk = 0
                mcb = Buf('mc2')
                mcfb = Buf('mcf2')
                masks = self.sb(S, 'masks', [128, 4, 512], BF16)
                PM = self.sb(S, 'PM', [128, 36, 128], BF16)
                wsT = self.sb(S, 'wsT', [128, 4, 128], BF16)
                bbc = self.sb(S, 'bbc', [64, 4, 128], F32)
                T.dma('pool', PM[:, 0:20, :], dram['PM'], 'mc', writes=[mcb])
                T.dma('pool', wsT[:, :, :], dram['wsT'][l], 'mc', writes=[mcb])
                T.dma('sp', bbc[:, :, :], dram['bbc'][l], 'mcf', writes=[mcfb])
                if is_s:
                    ck = self.sb(S, 'ck', [64, 2, 512], BF16)
                    cvt = self.sb(S, 'cvt', [128, 4, 2, 128], BF16)
                    T.dma('pool', masks[:, 0:2, :], dram['masks'], 'mc', writes=[mcb])
                    T.dma('pool', masks[:, 2:4, :], dram['masko'], 'mc', writes=[mcb])
                    T.dma('pool', PM[:, 20:36, :], dram['PMo'], 'mc', writes=[mcb])
                    T.dma('pool', ck[:, :, :], dram['ckT'][l], 'mc', writes=[mcb])
                    T.dma('pool', cvt[:, :, :, :], dram['cv'][l], 'mc', writes=[mcb])
                mcb.w = ('mc', T.dsem['mc'][1])
                mixed = self.sb(S, 'mixed', [128, 8, 512], BF16)
                mixb = [Buf('mix%d' % j) for j in range(8)]
                PT = [self.sb(S, 'PT%d' % i, [128, 512], BF16) for i in range(4)]
                PTb = [Buf('PT%d' % i) for i in range(4)]
                den = [self.sb(S, 'den%d' % i, [64, 512], F32) for i in range(2)]
                denb = [Buf('den0'), Buf('den1')]
                zt = [self.sb(S, 'zt%d' % i, [128, 512], F32) for i in range(2)]
                ztb = [Buf('zt0'), Buf('zt1')]
                dbf = [self.sb(S, 'dbf%d' % i, [64, 512], BF16) for i in range(2)]
                dbfb = [Buf('dbf0'), Buf('dbf1')]
                wot = [self.sb(S, 'wot%d' % i, [128, NKC, 512], BF16) for i in range(2)]
                wotb = [Buf('wot%d' % i) for i in range(2)]
                wos = ['woa', 'wob']
                f = self.sb(S, 'fm', [128, NKC, 512], F32)
                wosrc = dram['w_out'][l].rearrange("(kc p) n -> p kc n", p=128)
                tiles = split_tiles(fullb, 4)
                seqw = [(ti, hf) for ti in range(len(tiles)) for hf in range(2)]

                def loadwo(i):
                    ti, hf = seqw[i]
                    T.dma('pool', wot[i % 2][:, :, :], wosrc[:, :, 512 * hf:512 * hf + 512], wos[i % 2],
                          writes=[wotb[i % 2]])

                for i in range(min(2, len(seqw))):
                    loadwo(i)
                uk = 0
                pk = 0
                for ti, t in enumerate(tiles):
                    n = len(t) * 128
                    xo = t[0] * 128
                    xbufs = [c_ for b in t for c_ in self.xcb[b]]
                    units = []
                    for bi, b in enumerate(t):
                        for hk in range(2):
                            keys = []
                            if is_s:
                                mp = masks[:, 2, :] if b == 4 else masks[:, 0, :]
                                mn = masks[:, 3, :] if b == 7 else masks[:, 1, :]
                                keys.append((kT[:, hk, 128 * (b - 1):128 * b], [kb[b - 1]],
                                             vtm[:, b - 1, hk, :], [vb[b - 1]], mp))
                                keys.append((kT[:, hk, 128 * b:128 * (b + 1)], [kb[b]],
                                             vtm[:, b, hk, :], [vb[b]], None))
                                keys.append((kT[:, hk, 128 * (b + 1):128 * (b + 2)], [kb[b + 1]],
                                             vtm[:, b + 1, hk, :], [vb[b + 1]], mn))
                                for c in range(4):
                                    keys.append((ck[:, hk, 128 * c:128 * c + 128], [mcb],
                                                 cvt[:, c, hk, :], [mcb], None))
                            else:
                                sb0 = (b // 2) * 2
                                for kb_ in (sb0, sb0 + 1):
                                    keys.append((kT[:, hk, 128 * kb_:128 * kb_ + 128], [kb[kb_]],
                                                 vtm[:, kb_, hk, :], [vb[kb_]], None))
                            units.append(dict(bi=bi, b=b, hk=hk, keys=keys, pso=None))
                    items = [(u, ki) for u in units for ki in range(len(u['keys']))]

                    def emit_scores(u, ki):
                        kap, kbufs, vap, vbufs, mask = u['keys'][ki]
                        b, hk = u['b'], u['hk']
                        if u['pso'] is None:
                            u['pso'] = self.bank((0, 1, 2, 3))
                        rhs_q = qT[:, 4 * hk:4 * hk + 4, 128 * b:128 * b + 128]
                        psc, pscb = self.bank((4, 5, 6, 7))
                        T.op('pe', lambda e: e.matmul(
                            psc[:, :].rearrange("p (g t) -> p g t", g=4), lhsT=kap, rhs=rhs_q,
                            start=True, stop=(mask is None)),
                            reads=kbufs + [qb[b]], writes=[pscb], inc=(mask is None))
                        if mask is not None:
                            T.op('pe', lambda e: e.matmul(
                                psc[:, :], lhsT=self.ident, rhs=mask, start=False, stop=True),
                                reads=[mcb, self.const_b], writes=[pscb], inc=True)
                        return psc, pscb

                    def emit_exp_pv(u, ki, psc, pscb):
                        nonlocal pk, uk
                        kap, kbufs, vap, vbufs, mask = u['keys'][ki]
                        nk = len(u['keys'])
                        pso, psob = u['pso']
                        pi = pk % 4
                        pk += 1
                        T.op('act', lambda e: e.activation(
                            out=PT[pi][:, :], in_=psc[:, :], func=AF.Exp, scale=SCALE),
                            reads=[pscb], writes=[PTb[pi]])
                        T.op('pe', lambda e: e.matmul(
                            pso[:, :], lhsT=vap, rhs=PT[pi][:, :], start=(ki == 0), stop=(ki == nk - 1)),
                            reads=vbufs + [PTb[pi]], writes=[psob], inc=(ki == nk - 1))
                        if ki == nk - 1:
                            hk, bi = u['hk'], u['bi']
                            di = uk % 2
                            uk += 1
                            T.op('dve', lambda e: e.tensor_tensor(
                                out=den[di][:, :].rearrange("p (g t) -> p g t", g=4),
                                in0=pso[64:128, :].rearrange("p (g t) -> p g t", g=4),
                                in1=self.esink[:, l, 4 * hk:4 * hk + 4].unsqueeze(2).to_broadcast([64, 4, 128]),
                                op=ALU.add), reads=[psob, self.constf_b], writes=[denb[di]])
                            T.op('act', lambda e: e.activation(out=den[di][:, :], in_=den[di][:, :], func=AF.Ln),
                                 reads=[denb[di]], writes=[denb[di]])
                            T.op('act', lambda e: e.activation(out=den[di][:, :], in_=den[di][:, :], func=AF.Exp,
                                                               scale=-1.0),
                                 reads=[denb[di]], writes=[denb[di]])
                            for par in range(2):
                                T.op('dve', lambda e: e.tensor_tensor(
                                    out=mixed[64 * par:64 * par + 64, 2 * hk:2 * hk + 2, 128 * bi:128 * bi + 128],
                                    in0=pso[0:64, :].rearrange("p (gg par t) -> p gg par t", par=2, t=128)[:, :, par, :],
                                    in1=den[di][:, :].rearrange("p (gg par t) -> p gg par t", par=2, t=128)[:, :, par, :],
                                    op=ALU.mult),
                                    reads=[psob, denb[di]], writes=mixb[2 * hk:2 * hk + 2])

                    prev = None
                    for (u, ki) in items:
                        sc = emit_scores(u, ki)
                        if prev is not None:
                            emit_exp_pv(*prev)
                        prev = (u, ki, sc[0], sc[1])
                    if prev is not None:
                        emit_exp_pv(*prev)
                    for hh in range(4):
                        psz, pszb = self.bank((4, 5, 6, 7))
                        for bi, b in enumerate(t):
                            T.op('pe', lambda e: e.matmul(
                                psz[0:64, 128 * bi:128 * bi + 128], lhsT=vh[:, b, 64 * hh:64 * hh + 64],
                                rhs=wsT[:, hh, :], start=True, stop=True),
                                reads=[vhb[b], mcb], writes=[pszb], inc=(bi == len(t) - 1))
                        zi = hh % 2
                        po = 64 * (hh % 2)
                        T.op('dve', lambda e: e.tensor_tensor(
                            out=zt[zi][po:po + 64, 0:n].rearrange("p (b t) -> p b t", t=128),
                            in0=psz[0:64, 0:n].rearrange("p (b t) -> p b t", t=128),
                            in1=bbc[:, hh, :].unsqueeze(1).to_broadcast([64, len(t), 128]), op=ALU.add),
                            reads=[pszb, mcfb], writes=[ztb[zi]])
                        T.op('dve', lambda e: e.tensor_tensor(
                            out=mixed[po:po + 64, 4 + hh // 2, 0:n], in0=zt[zi][po:po + 64, 0:n],
                            in1=guT[po:po + 64, hh // 2, xo:xo + n], op=ALU.mult),
                            reads=[ztb[zi]] + [gub[b] for b in t], writes=[mixb[4 + hh // 2]])
                    for g in range(4):
                        psd, psdb = self.bank((4, 5, 6, 7))
                        for bi, b in enumerate(t):
                            if is_s:
                                if b == 4:
                                    nbrs = [(b - 1, 20 + g), (b, 24 + g), (b + 1, 16 + g)]
                                elif b == 7:
                                    nbrs = [(b - 1, 0 + g), (b, 28 + g), (b + 1, 32 + g)]
                                else:
                                    nbrs = [(b - 1, 0 + g), (b, 4 + g), (b + 1, 16 + g)]
                            else:
                                if b % 2 == 0:
                                    nbrs = [(b, 8 + g), (b + 1, 16 + g)]
                                else:
                                    nbrs = [(b - 1, 0 + g), (b, 12 + g)]
                            for ni, (nbk, pmi) in enumerate(nbrs):
                                nn = len(nbrs)
                                T.op('pe', lambda e: e.matmul(
                                    psd[0:64, 128 * bi:128 * bi + 128], lhsT=pltm[:, nbk, 64 * g:64 * g + 64],
                                    rhs=PM[:, pmi, :], start=(ni == 0), stop=(ni == nn - 1)),
                                    reads=[plb[nbk], mcb], writes=[psdb],
                                    inc=(bi == len(t) - 1 and ni == len(nbrs) - 1))
                        dj = g % 2
                        po = 64 * (g % 2)
                        T.op('act', lambda e: e.activation(out=dbf[dj][:, 0:n], in_=psd[0:64, 0:n], func=AF.Copy),
                             reads=[psdb], writes=[dbfb[dj]])
                        psp, pspb = self.bank((4, 5, 6, 7))
                        T.op('pe', lambda e: e.matmul(
                            psp[0:64, 0:n], lhsT=self.wpool[:, l, g, :], rhs=dbf[dj][:, 0:n], start=True, stop=True),
                            reads=[dbfb[dj], self.const_b], writes=[pspb], inc=True)
                        T.op('dve', lambda e: e.tensor_scalar(
                            out=mixed[po:po + 64, 6 + g // 2, 0:n], in0=psp[0:64, 0:n],
                            scalar1=self.pscale[:, l, g:g + 1], scalar2=None, op0=ALU.mult),
                            reads=[pspb, self.constf_b], writes=[mixb[6 + g // 2]])
                    fb = [Buf('fm%d' % m) for m in range(NKC)]
                    nbk = 7 - (ti % 2)
                    pnt, pnb = self.ps[nbk], self.psb[nbk]
                    pend = []

                    def flush(keep):
                        while len(pend) > keep:
                            m_, qi_ = pend.pop(0)
                            T.op('pe', lambda e: e.matmul(pnt[:, 0:n], lhsT=self.meanm, rhs=sqf[qi_][:, 0:n],
                                                          start=(m_ == 0), stop=(m_ == NKC - 1)),
                                 reads=[sqfb[qi_], self.const_b], writes=[pnb], inc=True)

                    for m in range(NKC):
                        i = ti * 2 + m // 4
                        wi = i % 2
                        psf, psfb = self.bank((0, 1, 2, 3))
                        for kc in range(NKC):
                            T.op('pe', lambda e: e.matmul(
                                psf[:, 0:n], lhsT=wot[wi][:, kc, 128 * (m % 4):128 * (m % 4) + 128],
                                rhs=mixed[:, kc, 0:n], start=(kc == 0), stop=(kc == NKC - 1)),
                                reads=[wotb[wi], mixb[kc]], writes=[psfb], inc=(kc == NKC - 1))
                        flush(1)
                        qi = qk % 4
                        qk += 1
                        T.op('act', lambda e: e.activation(out=sqf[qi][:, 0:n], in_=psf[:, 0:n], func=AF.Square),
                             reads=[psfb], writes=[sqfb[qi]])
                        T.op('dve', lambda e: e.tensor_scalar(
                            out=f[:, m, 0:n], in0=psf[:, 0:n], scalar1=self.coefG[:, l, 1, m, cv:cv + 1],
                            scalar2=None, op0=ALU.mult), reads=[psfb, self.coef_b], writes=[fb[m]])
                        pend.append((m, qi))
                        if m % 4 == 3 and i + 2 < len(seqw):
                            loadwo(i + 2)
                    flush(0)
                    T.op('act', lambda e: e.activation(out=rstdm[:, 0:n], in_=pnt[:, 0:n], func=AF.Ln,
                                                       bias=self.epsc[:, 0:1], scale=1.0),
                         reads=[pnb, self.constf_b], writes=[rstdmb])
                    T.op('act', lambda e: e.activation(out=rstdm[:, 0:n], in_=rstdm[:, 0:n], func=AF.Exp, scale=-0.5),
                     reads=[rstdmb], writes=[rstdmb])
                    for m in range(NKC + 1):
                        if m < NKC:
                            tb, tbb = tmpm[m % 4], tmpmb[m % 4]
                            T.op('dve', lambda e: e.tensor_tensor(out=tb[:, 0:n], in0=f[:, m, 0:n],
                                                                  in1=rstdm[:, 0:n], op=ALU.mult),
                                 reads=[fb[m], rstdmb], writes=[tbb])
                        if m >= 1:
                            m1 = m - 1
                            tb1, tbb1 = tmpm[m1 % 4], tmpmb[m1 % 4]
                            T.op('dve', lambda e: e.tensor_tensor(out=x[:, m1, xo:xo + n], in0=x[:, m1, xo:xo + n],
                                                                  in1=tb1[:, 0:n], op=ALU.add),
                                 reads=[tbb1], writes=[self.xcb[b_][m1] for b_ in t])
                T.barrier()


def _pool_mats():
    PM = np.zeros((128, 20, 128), np.float32)
    tp = np.arange(128)[:, None]
    t = np.arange(128)[None, :]
    for g, w in enumerate(POOL_WINDOWS):
        hf = w // 2
        eye = (tp == t).astype(np.float32)
        same = ((tp >= t - hf) & (tp < t + hf)).astype(np.float32)
        prev = ((tp - 128) >= (t - hf)).astype(np.float32)
        nxt = ((tp + 128) < (t + hf)).astype(np.float32)
        PM[:, 0 + g, :] = prev / w
        PM[:, 4 + g, :] = same / w - eye
        cnt_first = (np.minimum(t + hf, 128 + hf) - np.maximum(t - hf, 0)).astype(np.float32)
        PM[:, 8 + g, :] = same / cnt_first - eye
        cnt_last = (np.minimum(t + hf, 128) - (t - hf)).astype(np.float32)
        PM[:, 12 + g, :] = same / cnt_last - eye
        PM[:, 16 + g, :] = nxt / w
    return PM


def _win_cols():
    def perm64(base):
        idx = np.arange(64)
        half = idx % 32
        src = np.where(half < 16, idx + 16, idx - 16)
        return base + src
    cols = []
    for st in range(4):
        cols.append(np.arange(128) + 2 * st * 64)
        cols.append(perm64(2 * st * 64))
        cols.append(perm64((2 * st + 1) * 64))
    cols.append(np.arange(128) + 512)
    cols.append(perm64(512))
    cols.append(perm64(512 + 64))
    for hh in range(4):
        cols.append(np.arange(64) + 768 + hh * 64)
    cols.append(np.arange(128) + 640)
    cols.append(np.arange(256) + 1024)
    cols.append(np.arange(256) + 1280)
    return np.concatenate(cols)


_NC_CACHE = {}


def _get_nc(cfg):
    key = tuple(sorted(cfg.items()))
    if key not in _NC_CACHE:
        _NC_CACHE[key] = Builder(dict(cfg)).build()
    return _NC_CACHE[key]


def kernel(x_prompt, x_sample, cache_k, cache_v, c, c_ctx, w_mod, b_mod, norm_w, w_in, w_out,
           attn_sink, w_spatial, b_spatial, w_pool, pool_scale, ffn_w1, ffn_w2, _cfg=None):
    cfg = dict(CFG) if _cfg is None else dict(_cfg)
    f32 = np.float32
    A = lambda a: np.ascontiguousarray(np.asarray(a, dtype=f32))
    x_prompt, x_sample, cache_k, cache_v = A(x_prompt), A(x_sample), A(cache_k), A(cache_v)
    c, c_ctx = A(c), A(c_ctx)
    shared = {}
    shared['w_mod'] = A(w_mod)
    shared['w1'] = A(ffn_w1)
    shared['w2'] = A(ffn_w2)
    shared['w_in'] = A(np.asarray(w_in, f32)[:, :, _win_cols()])
    shared['w_out'] = A(w_out)
    shared['nw'] = A(np.asarray(norm_w, f32).reshape(L, 6, NKC, 128).transpose(3, 0, 1, 2))
    shared['bmod'] = A(np.asarray(b_mod, f32).reshape(L, 72, 128).transpose(2, 0, 1))
    shared['wsT'] = A(np.asarray(w_spatial, f32).transpose(0, 3, 1, 2))
    shared['bbc'] = A(np.broadcast_to(np.asarray(b_spatial, f32)[:, None, :, :], (L, 64, 4, 128)))
    shared['sinkb'] = A(np.broadcast_to(np.asarray(attn_sink, f32)[None], (64, L, 8)))
    shared['pscale'] = A(np.asarray(pool_scale, f32).reshape(L, 4, 64).transpose(2, 0, 1))
    shared['wpool'] = A(np.asarray(w_pool, f32).transpose(2, 0, 1, 3))
    cmat = np.zeros((128, 320), f32)
    cmat[:, 0:128] = np.eye(128, dtype=f32)
    cmat[:, 128:256] = 1.0 / 1024.0
    cmat[:, 256:320] = 1.0
    shared['cmat'] = cmat
    kk = np.arange(128)[:, None]
    qq = np.arange(128)[None, :]
    maskL = np.where(kk >= qq, 0.0, NEG).astype(f32)
    maskU = np.where(kk <= qq, 0.0, NEG).astype(f32)
    dead = np.full((128, 128), NEG, f32)
    tile4 = lambda m: np.tile(m, (1, 4))
    shared['masks'] = A(np.stack([tile4(maskL), tile4(maskU)], axis=1))
    PM = _pool_mats()
    shared['PM'] = PM
    nf = 16
    inv = 10000.0 ** (-np.arange(nf, dtype=np.float64) / nf)

    in_maps = []
    for core in range(NCORES):
        m = dict(shared)
        xp = x_prompt[4 * core:4 * core + 4].reshape(NPB * 128, D)
        m['xp'] = A(xp.T.reshape(NKC, 128, NPB * 128).transpose(1, 0, 2))
        bidx = core // 4
        start = 512 * (core % 4)
        win = np.zeros((NSB * 128, D), f32)
        lo, hi = start - 512, start + 1024
        slo, shi = max(lo, 0), min(hi, 2048)
        win[slo - lo:shi - lo] = x_sample[bidx, slo:shi]
        m['xs'] = A(win.T.reshape(NKC, 128, NSB * 128).transpose(1, 0, 2))
        m['cT'] = A(np.stack([c_ctx, c[bidx]], axis=1).reshape(NKC, 128, 2).transpose(1, 0, 2))
        pos = np.clip(np.arange(lo, hi), 0, 2047)
        row = (pos // 64).astype(np.float64)
        col = (pos % 64).astype(np.float64)
        ang_r = row[None, :] * inv[:, None]
        ang_c = col[None, :] * inv[:, None]
        cosr, sinr, cosc, sinc = np.cos(ang_r), np.sin(ang_r), np.cos(ang_c), np.sin(ang_c)
        Ct = np.concatenate([cosr, cosr, cosc, cosc] * 2, axis=0).astype(f32)
        St = np.concatenate([-sinr, sinr, -sinc, sinc] * 2, axis=0).astype(f32)
        m['rope'] = A(np.stack([Ct, St], axis=1))
        first = (core % 4 == 0)
        last = (core % 4 == 3)
        mP = tile4(dead) if first else tile4(maskL)
        mN = tile4(dead) if last else tile4(maskU)
        m['masko'] = A(np.stack([mP, mN], axis=1))
        PMo = np.zeros((128, 16, 128), f32)
        PMo[:, 0:4] = 0.0 if first else PM[:, 0:4]
        PMo[:, 4:8] = PM[:, 8:12] if first else PM[:, 4:8]
        PMo[:, 8:12] = PM[:, 12:16] if last else PM[:, 4:8]
        PMo[:, 12:16] = 0.0 if last else PM[:, 16:20]
        m['PMo'] = A(PMo)
        m['ckT'] = A(cache_k[bidx].transpose(0, 3, 2, 1))
        cva = np.ones((L, 128, 4, 2, 128), f32)
        cva[..., 0:64] = cache_v[bidx].reshape(L, 4, 128, 2, 64).transpose(0, 2, 1, 3, 4)
        m['cv'] = cva
        in_maps.append(m)

    nc = _get_nc(cfg)
    res = run_bass_kernel_spmd(nc, in_maps, core_ids=list(range(NCORES)))
    y_prompt = np.zeros((32, 256, D), f32)
    y_sample = np.zeros((2, 2048, D), f32)
    nk = np.zeros((32, L, 256, 2, 64), f32)
    nv = np.zeros((32, L, 256, 2, 64), f32)
    for core in range(NCORES):
        r = res.results[core]
        yp = np.asarray(r['yp']).transpose(1, 0, 2).reshape(D, NPB * 128).T
        y_prompt[4 * core:4 * core + 4] = yp.reshape(4, 256, D)
        ys = np.asarray(r['ys']).transpose(1, 0, 2).reshape(D, 512).T
        y_sample[core // 4, 512 * (core % 4):512 * (core % 4) + 512] = ys
        k_ = np.asarray(r['nkT']).reshape(L, 2, 64, NPB * 128)
        nk[4 * core:4 * core + 4] = k_.transpose(3, 0, 1, 2).reshape(4, 256, L, 2, 64).transpose(0, 2, 1, 3, 4)
        v_ = np.asarray(r['nv'])
        nv[4 * core:4 * core + 4] = v_.reshape(L, 4, 256, 2, 64).transpose(1, 0, 2, 3, 4)
    return (y_prompt, y_sample, nk, nv)
```

```python
import types
import numpy as np
from contextlib import ExitStack
import concourse.bass as bass
import concourse.mybir as mybir
from concourse.bass_utils import run_bass_kernel_spmd

F32 = mybir.dt.float32
BF16 = mybir.dt.bfloat16
AF = mybir.ActivationFunctionType
ALU = mybir.AluOpType
AX = mybir.AxisListType

NCORES = 8
L = 4
D = 1024
DFF = 2816
NKC = 8
NHC = 22
NMOD = 9
EPS = 1e-6
SCALE = 0.125
NEG = -30000.0
WIN_COLS = 2176
NPB = 8
NSB = 12
ENG = ('pe', 'act', 'dve', 'pool', 'sp')
POOL_WINDOWS = (2, 4, 8, 16)

CFG = dict(depth=4, prompt=True, sample=True, mixer=True, ffn2=True, ffn1=True, mix_stop=0)


class Buf:
    __slots__ = ('name', 'w', 'r', 'excl', 'wread')

    def __init__(self, name='', excl=False):
        self.name = name
        self.w = None
        self.r = {}
        self.wread = False
        self.excl = excl


def _freeze(fn):
    if fn.__closure__ is None:
        return fn
    cells = []
    for c in fn.__closure__:
        try:
            cells.append(types.CellType(c.cell_contents))
        except ValueError:
            cells.append(c)
    return types.FunctionType(fn.__code__, fn.__globals__, fn.__name__, fn.__defaults__, tuple(cells))


class Tracker:
    def __init__(self, nc, stack):
        self.nc = nc
        self.stack = stack
        self.ops = {e: [] for e in ENG}
        self.sem = {e: stack.enter_context(nc.semaphore('s_' + e)) for e in ENG}
        self.cnt = {e: 0 for e in ENG}
        self.waited = {e: {} for e in ENG}
        self.dsem = {}
        self.nops = 0

    def new_dsem(self, name):
        h = self.stack.enter_context(self.nc.semaphore('d_' + name))
        self.dsem[name] = [h, 0]
        return name

    def _h(self, key):
        return self.sem[key] if key in self.sem else self.dsem[key][0]

    def _wait(self, e, tok):
        if tok is None:
            return
        key, val = tok
        if e == 'pe' and key == 'pe':
            return
        if self.waited[e].get(key, 0) >= val:
            return
        self.waited[e][key] = val
        h = self._h(key)
        self.ops[e].append(lambda eng, h=h, val=val: eng.wait_ge(h, val))

    def _deps(self, e, reads, writes):
        for b in reads:
            self._wait(e, b.w)
        for b in writes:
            if not (b.w is not None and b.w[0] == e and not b.r and b not in reads):
                self._wait(e, b.w)
            for k, v in b.r.items():
                self._wait(e, (k, v))

    def _mark(self, tok, reads, writes):
        k, v = tok
        for b in reads:
            if b.r.get(k, 0) < v:
                b.r[k] = v
        for b in writes:
            b.w = tok
            b.r = {}

    def op(self, e, fn, reads=(), writes=(), inc=True):
        fn = _freeze(fn)
        promoted = [b for b in reads if b.excl]
        if promoted:
            reads = [b for b in reads if not b.excl]
            for b in promoted:
                if not (b.wread and b.w is not None and b.w[0] == e):
                    self._wait(e, b.w)
                for k, v in b.r.items():
                    self._wait(e, (k, v))
        self._deps(e, reads, writes)
        self.nops += 1
        if inc:
            self.cnt[e] += 1
            tok = (e, self.cnt[e])
            h = self.sem[e]
            self.ops[e].append(lambda eng, fn=fn, h=h: fn(eng).then_inc(h, 1))
        else:
            tok = (e, self.cnt[e] + 1)
            self.ops[e].append(lambda eng, fn=fn: fn(eng))
        self._mark(tok, reads, writes)
        for b in writes:
            b.wread = False
        for b in promoted:
            b.w = tok
            b.r = {}
            b.wread = True
        return tok

    def dma(self, q, out_ap, in_ap, dsem, reads=(), writes=()):
        for b in reads:
            self._wait(q, b.w)
        for b in writes:
            if not (b.w is not None and b.w[0] == dsem):
                self._wait(q, b.w)
            for k, v in b.r.items():
                self._wait(q, (k, v))
        self.dsem[dsem][1] += 16
        tok = (dsem, self.dsem[dsem][1])
        h = self.dsem[dsem][0]
        self.ops[q].append(lambda eng, o=out_ap, i=in_ap, h=h: eng.dma_start(out=o, in_=i).then_inc(h, 16))
        self._mark(tok, reads, writes)
        return tok

    def barrier(self):
        toks = [(e, self.cnt[e]) for e in ENG if self.cnt[e] > 0]
        toks += [(k, v[1]) for k, v in self.dsem.items() if v[1] > 0]
        for e in ENG:
            for t in toks:
                if t[0] != e or e in ('act', 'dve', 'pool'):
                    self._wait(e, t)


def split_tiles(blocks, maxb=4):
    n = len(blocks)
    if n == 0:
        return []
    nt = (n + maxb - 1) // maxb
    base, rem = divmod(n, nt)
    out, i = [], 0
    for t in range(nt):
        k = base + (1 if t < rem else 0)
        out.append(blocks[i:i + k])
        i += k
    return out


def make_groups(tiles, maxblocks):
    groups, cur, nb = [], [], 0
    for t in tiles:
        if cur and nb + len(t) > maxblocks:
            groups.append(cur)
            cur, nb = [], 0
        cur.append(t)
        nb += len(t)
    if cur:
        groups.append(cur)
    return groups


class Builder:
    def __init__(self, cfg):
        self.cfg = cfg
        self.uid = 0

    def sb(self, scope, name, shape, dt):
        self.uid += 1
        return scope.enter_context(self.nc.sbuf_tensor('%s_%d' % (name, self.uid), list(shape), dt))

    def build(self):
        cfg = self.cfg
        nc = bass.Bass("TRN2", target_bir_lowering=False)
        self.nc = nc
        dram = {}

        def din(name, shape):
            dram[name] = nc.dram_tensor(name, list(shape), F32, kind="ExternalInput").ap()

        def dout(name, shape):
            dram[name] = nc.dram_tensor(name, list(shape), F32, kind="ExternalOutput").ap()

        din('xp', [128, NKC, NPB * 128])
        din('xs', [128, NKC, NSB * 128])
        din('cT', [128, NKC, 2])
        din('rope', [128, 2, NSB * 128])
        din('masko', [128, 2, 512])
        din('PMo', [128, 16, 128])
        din('ckT', [L, 64, 2, 512])
        din('cv', [L, 128, 4, 2, 128])
        din('w_mod', [L, D, NMOD * D])
        din('w1', [L, 2, D, 2 * DFF])
        din('w2', [L, 2, DFF, D])
        din('w_in', [L, D, WIN_COLS])
        din('w_out', [L, D, D])
        din('nw', [128, L, 6, NKC])
        din('bmod', [128, L, 72])
        din('wsT', [L, 128, 4, 128])
        din('bbc', [L, 64, 4, 128])
        din('sinkb', [64, L, 8])
        din('pscale', [64, L, 4])
        din('wpool', [64, L, 4, 64])
        din('cmat', [128, 320])
        din('masks', [128, 2, 512])
        din('PM', [128, 20, 128])
        dout('yp', [128, NKC, NPB * 128])
        dout('ys', [128, NKC, 512])
        dout('nkT', [L, 128, NPB * 128])
        dout('nv', [L, NPB * 128, 128])
        self.dram = dram

        with ExitStack() as stack:
            T = Tracker(nc, stack)
            self.T = T
            for n in ('const', 'constf', 'mc', 'mcf', 'x', 'y', 'w1a', 'w1b', 'w2a', 'w2b', 'w2c', 'wia', 'wib', 'wtm',
                      'woa', 'wob', 'wma', 'wmb', 'wmc', 'wmd', 'ko', 'vo', 'cache'):
                T.new_dsem(n)
            self.ps = []
            self.psb = []
            for i in range(8):
                self.ps.append(stack.enter_context(nc.psum_tensor('ps%d' % i, [128, 512], F32)))
                self.psb.append(Buf('ps%d' % i, excl=True))
            self.rot = {}
            P = stack
            self.x = self.sb(P, 'x', [128, NKC, NSB * 128], F32)
            self.xcb = [[Buf('x%d_%d' % (b, kc)) for kc in range(NKC)] for b in range(NSB)]
            self.coefA = self.sb(P, 'coefA', [128, L, 3, NKC, 2], F32)
            self.coefB = self.sb(P, 'coefB', [128, L, 3, NKC, 2], F32)
            self.coefG = self.sb(P, 'coefG', [128, L, 3, NKC, 2], F32)
            self.coef_b = Buf('coef')
            self.cm = self.sb(P, 'cm', [128, 320], BF16)
            self.esink = self.sb(P, 'esink', [64, L, 8], F32)
            self.pscale = self.sb(P, 'pscale', [64, L, 4], F32)
            self.wpool = self.sb(P, 'wpool', [64, L, 4, 64], BF16)
            self.const_b = Buf('const')
            self.constf_b = Buf('constf')
            self.epsc = self.sb(P, 'epsc', [128, 1], F32)
            self.ident = self.cm[:, 0:128]
            self.meanm = self.cm[:, 128:256]
            self.ones64 = self.cm[:, 256:320]

            if cfg['prompt']:
                T.dma('sp', self.x[:, :, 0:NPB * 128], dram['xp'], 'x',
                      writes=[c_ for b in range(NPB) for c_ in self.xcb[b]])
            self.prologue()
            depth = cfg['depth']
            if cfg['prompt']:
                self.run_supergroup('p', depth)
            if cfg['sample']:
                self.run_supergroup('s', depth)
            T.barrier()

            with nc.Block() as block:
                @block.tensor
                def _(e):
                    for f in T.ops['pe']:
                        f(e)

                @block.scalar
                def _(e):
                    for f in T.ops['act']:
                        f(e)

                @block.vector
                def _(e):
                    for f in T.ops['dve']:
                        f(e)

                @block.gpsimd
                def _(e):
                    for f in T.ops['pool']:
                        f(e)

                @block.sync
                def _(e):
                    for f in T.ops['sp']:
                        f(e)
        return nc

    def bank(self, pool):
        i = self.rot.get(pool, 0)
        self.rot[pool] = i + 1
        b = pool[i % len(pool)]
        return self.ps[b], self.psb[b]

    def prologue(self):
        T, nc, dram = self.T, self.nc, self.dram
        x = self.x
        depth = self.cfg['depth']
        T.dma('pool', self.cm[:, :], dram['cmat'], 'const', writes=[self.const_b])
        T.dma('pool', self.wpool[:, :, :, :], dram['wpool'], 'const', writes=[self.const_b])
        T.dma('sp', self.esink[:, :, :], dram['sinkb'], 'constf', writes=[self.constf_b])
        T.dma('sp', self.pscale[:, :, :], dram['pscale'], 'constf', writes=[self.constf_b])
        modT = [x[:, l, 1280:1424].rearrange("p (j c) -> p j c", c=2) for l in range(L)]
        nwv = [x[:, l, 1424:1472].rearrange("p (i k) -> p i k", k=NKC) for l in range(L)]
        bmv = [x[:, 4 + l, 1280:1352] for l in range(L)]
        cTv = x[:, 4, 1360:1376]
        scv = x[:, 5, 1360:1368].bitcast(BF16).rearrange("p (k c) -> p k c", c=2)
        NWB = 4
        wmv = [x[:, :, 1024 + 64 * i:1024 + 64 * (i + 1)].bitcast(BF16) for i in range(NWB)]
        T.dma('sp', cTv, dram['cT'].rearrange("p k c -> p (k c)"), 'constf', writes=[self.constf_b])
        T.dma('sp', x[:, 0:L, 1424:1472], dram['nw'].rearrange("p l i k -> p l (i k)"), 'constf',
              writes=[self.constf_b])
        T.dma('sp', x[:, 4:4 + L, 1280:1352], dram['bmod'], 'constf', writes=[self.constf_b])
        self.const_b.w = ('const', T.dsem['const'][1])
        self.constf_b.w = ('constf', T.dsem['constf'][1])
        cb = self.constf_b
        T.op('dve', lambda e: e.memset(self.epsc[:, :], EPS), writes=[cb])
        T.op('act', lambda e: e.activation(out=self.esink[:, :, :], in_=self.esink[:, :, :], func=AF.Exp),
             reads=[cb], writes=[cb])
        scb = Buf('sc')
        T.op('act', lambda e: e.activation(out=scv, in_=cTv.rearrange("p (k c) -> p k c", c=2), func=AF.Silu),
             reads=[cb], writes=[scb])
        modb = [Buf('modT%d' % l) for l in range(L)]
        wmb = [Buf('wm%d' % i) for i in range(NWB)]
        wsem = ['wma', 'wmb', 'wmc', 'wmd']
        NST = 72
        seq = [(l, st) for l in range(depth) for st in range(NST)]

        def load(i):
            l, st = seq[i]
            src = dram['w_mod'][l].rearrange("(kc p) n -> p kc n", p=128)[:, :, 128 * st:128 * st + 128]
            T.dma('pool', wmv[i % NWB], src, wsem[i % NWB], writes=[wmb[i % NWB]])

        def step(i):
            l, st = seq[i]
            pst, psb = self.bank(self.bg_pool)
            for kc in range(NKC):
                T.op('pe', lambda e: e.matmul(
                    pst[:, 0:2], lhsT=wmv[i % NWB][:, kc, :],
                    rhs=scv[:, kc, :], start=(kc == 0), stop=(kc == NKC - 1)),
                    reads=[wmb[i % NWB], scb], writes=[psb], inc=(kc == NKC - 1))
            if i + NWB < len(seq):
                load(i + NWB)
            T.op('dve', lambda e: e.tensor_tensor(
                out=modT[l][:, st, :], in0=pst[:, 0:2], in1=bmv[l][:, st:st + 1].to_broadcast([128, 2]),
                op=ALU.add), reads=[psb, cb], writes=[modb[l]])
            if st == NST - 1:
                for s_ in range(3):
                    sh, scl, gt = 3 * s_, 3 * s_ + 1, 3 * s_ + 2
                    npre, npost = 2 * s_, 2 * s_ + 1
                    gmul = 1.0 if s_ == 1 else 0.5
                    A = self.coefA[:, l, s_, :, :]
                    Bc = self.coefB[:, l, s_, :, :]
                    G = self.coefG[:, l, s_, :, :]
                    T.op('dve', lambda e: e.scalar_tensor_tensor(
                        out=A, in0=modT[l][:, scl * 8:(scl + 1) * 8, :], scalar=1.0,
                        in1=nwv[l][:, npre, :].unsqueeze(2).to_broadcast([128, NKC, 2]),
                        op0=ALU.add, op1=ALU.mult), reads=[modb[l], cb], writes=[self.coef_b])
                    T.op('dve', lambda e: e.tensor_copy(out=Bc, in_=modT[l][:, sh * 8:(sh + 1) * 8, :]),
                         reads=[modb[l]], writes=[self.coef_b])
                    T.op('dve', lambda e: e.scalar_tensor_tensor(
                        out=G, in0=modT[l][:, gt * 8:(gt + 1) * 8, :], scalar=gmul,
                        in1=nwv[l][:, npost, :].unsqueeze(2).to_broadcast([128, NKC, 2]),
                        op0=ALU.mult, op1=ALU.mult), reads=[modb[l], cb], writes=[self.coef_b])

        self.bg_pool = (6, 7)
        for i in range(min(NWB, len(seq))):
            load(i)
        for i in range(min(NST, len(seq))):
            step(i)
        self.bg = [(lambda i=i: step(i)) for i in range(NST, len(seq))]
        T.barrier()

    def bg_step(self, n):
        for _ in range(n):
            if self.bg:
                self.bg.pop(0)()

    def prenorm(self, tile_cols, n, xbufs, h, hoff, hbuf, l, s, cv, scr):
        T = self.T
        sq, sqb, rstd, rstdb, tmp, tmpb = scr
        x = self.x
        xo = tile_cols
        for kc in range(NKC):
            if kc % 2 == 0:
                T.op('act', lambda e, kc=kc: e.activation(out=sq[:, kc, 0:n], in_=x[:, kc, xo:xo + n], func=AF.Square),
                     reads=xbufs, writes=[sqb[kc]])
            else:
                T.op('dve', lambda e, kc=kc: e.tensor_tensor(out=sq[:, kc, 0:n], in0=x[:, kc, xo:xo + n],
                                                             in1=x[:, kc, xo:xo + n], op=ALU.mult),
                     reads=xbufs, writes=[sqb[kc]])
        pst, psb = self.bank((6, 7))
        for kc in range(NKC):
            T.op('pe', lambda e, kc=kc, pst=pst: e.matmul(pst[:, 0:n], lhsT=self.meanm, rhs=sq[:, kc, 0:n],
                                                          start=(kc == 0), stop=(kc == NKC - 1)),
                 reads=[sqb[kc], self.const_b], writes=[psb], inc=(kc == NKC - 1))
        T.op('act', lambda e, pst=pst: e.activation(out=rstd[:, 0:n], in_=pst[:, 0:n], func=AF.Ln,
                                                    bias=self.epsc[:, 0:1], scale=1.0), reads=[psb, self.constf_b], writes=[rstdb])
        T.op('act', lambda e: e.activation(out=rstd[:, 0:n], in_=rstd[:, 0:n], func=AF.Exp, scale=-0.5),
                     reads=[rstdb], writes=[rstdb])
        for kc in range(NKC):
            tb, tbb = tmp[kc % 2], tmpb[kc % 2]
            T.op('dve', lambda e, kc=kc, tb=tb: e.tensor_tensor(
                out=tb[:, 0:n], in0=x[:, kc, xo:xo + n], in1=rstd[:, 0:n], op=ALU.mult),
                reads=list(xbufs) + [rstdb], writes=[tbb])
            T.op('act', lambda e, kc=kc, tb=tb: e.activation(
                out=h[:, kc, hoff:hoff + n], in_=tb[:, 0:n], func=AF.Identity,
                bias=self.coefB[:, l, s, kc, cv:cv + 1], scale=self.coefA[:, l, s, kc, cv:cv + 1]),
                reads=[tbb, self.coef_b], writes=[hbuf])

    def postnorm(self, f, foff, fbufs, n, xo, xbufs, l, s, cv, scr):
        T = self.T
        sq, sqb, rstd, rstdb, tmp, tmpb = scr
        x = self.x
        for m in range(NKC):
            T.op('act', lambda e, m=m: e.activation(out=sq[:, m, 0:n], in_=f[:, m, foff:foff + n], func=AF.Square),
                 reads=[fbufs[m]], writes=[sqb[m]])
        pst, psb = self.bank((6, 7))
        for m in range(NKC):
            T.op('pe', lambda e, m=m, pst=pst: e.matmul(pst[:, 0:n], lhsT=self.meanm, rhs=sq[:, m, 0:n],
                                                        start=(m == 0), stop=(m == NKC - 1)),
                 reads=[sqb[m], self.const_b], writes=[psb], inc=(m == NKC - 1))
        T.op('act', lambda e, pst=pst: e.activation(out=rstd[:, 0:n], in_=pst[:, 0:n], func=AF.Ln,
                                                    bias=self.epsc[:, 0:1], scale=1.0), reads=[psb, self.constf_b], writes=[rstdb])
        T.op('act', lambda e: e.activation(out=rstd[:, 0:n], in_=rstd[:, 0:n], func=AF.Exp, scale=-0.5),
                     reads=[rstdb], writes=[rstdb])
        for m in range(NKC):
            tb, tbb = tmp[m % 2], tmpb[m % 2]
            T.op('dve', lambda e, m=m, tb=tb: e.scalar_tensor_tensor(
                out=tb[:, 0:n], in0=f[:, m, foff:foff + n], scalar=self.coefG[:, l, s, m, cv:cv + 1],
                in1=rstd[:, 0:n], op0=ALU.mult, op1=ALU.mult), reads=[fbufs[m], rstdb, self.coef_b], writes=[tbb])
            T.op('dve', lambda e, m=m, tb=tb: e.tensor_tensor(
                out=x[:, m, xo:xo + n], in0=x[:, m, xo:xo + n], in1=tb[:, 0:n], op=ALU.add),
                reads=[tbb], writes=xbufs)

    def norm_scratch(self, S):
        sq = self.sb(S, 'sq', [128, NKC, 512], BF16)
        sqb = [Buf('sq%d' % i) for i in range(NKC)]
        rstd = self.sb(S, 'rstd', [128, 512], F32)
        tmp = [self.sb(S, 'tmp%d' % i, [128, 512], F32) for i in range(2)]
        return (sq, sqb, rstd, Buf('rstd'), tmp, [Buf('tmp0'), Buf('tmp1')])

    def ffn(self, group, l, fi, s, cv):
        T, nc, dram = self.T, self.nc, self.dram
        x = self.x
        ntl = len(group)
        NT = sum(len(t) for t in group) * 128
        with ExitStack() as S:
            h = self.sb(S, 'h', [128, NKC, NT], BF16)
            hid = self.sb(S, 'hid', [128, NHC, NT], BF16)
            f = self.sb(S, 'f', [128, NKC, NT], F32)
            sq = self.sb(S, 'sq', [128, NKC, 512], BF16)
            sqb = [Buf('sq%d' % i) for i in range(NKC)]
            rstd = self.sb(S, 'rstd', [128, ntl, 512], F32)
            rstdb = [Buf('rstd%d' % i) for i in range(ntl)]
            tmp = [self.sb(S, 'tmp%d' % i, [128, 512], F32) for i in range(4)]
            tmpb = [Buf('tmp%d' % i) for i in range(4)]
            sg = [self.sb(S, 'sg%d' % i, [128, 512], F32) for i in range(2)]
            sgb = [Buf('sg0'), Buf('sg1')]
            sqf = [self.sb(S, 'sqf%d' % i, [128, 512], BF16) for i in range(4)]
            sqfb = [Buf('sqf%d' % i) for i in range(4)]
            w1t = [self.sb(S, 'w1t%d' % i, [128, NKC, 2, 256], BF16) for i in range(2)]
            w1b = [Buf('w1t0'), Buf('w1t1')]
            w2t = [self.sb(S, 'w2t%d' % i, [128, NHC, 128], BF16) for i in range(3)]
            w2b = [Buf('w2t%d' % i) for i in range(3)]
            w1s, w2s = ['w1a', 'w1b'], ['w2a', 'w2b', 'w2c']
            w1src = dram['w1'][l, fi].rearrange("(kc p) n -> p kc n", p=128)
            w2src = dram['w2'][l, fi].rearrange("(kc p) n -> p kc n", p=128)

            def load1(sp):
                i = sp % 2
                T.dma('pool', w1t[i][:, :, 0, :], w1src[:, :, 256 * sp:256 * sp + 256], w1s[i], writes=[w1b[i]])
                T.dma('pool', w1t[i][:, :, 1, :], w1src[:, :, DFF + 256 * sp:DFF + 256 * sp + 256], w1s[i],
                      writes=[w1b[i]])

            def load2(m):
                i = m % 3
                T.dma('pool', w2t[i][:, :, :], w2src[:, :, 128 * m:128 * m + 128], w2s[i], writes=[w2b[i]])

            load1(0)
            load1(1)
            scr = (sq, sqb, rstd[:, 0, :], rstdb[0], [tmp[0], tmp[1]], [tmpb[0], tmpb[1]])
            tiles = []
            ho = 0
            for ti, t in enumerate(group):
                n = len(t) * 128
                xo = t[0] * 128
                xbufs = [c_ for b in t for c_ in self.xcb[b]]
                hbuf = Buf('h')
                tiles.append((xo, n, ho, ho, xbufs, hbuf))
                ho += n

            def do_pre(ti):
                xo, n, ho, hc, xbufs, hbuf = tiles[ti]
                self.prenorm(xo, n, xbufs, h, ho, hbuf, l, s, cv, scr)

            do_pre(0)
            load2(0)
            load2(1)
            load2(2)
            hidb = [[Buf('hid') for _ in tiles] for _ in range(NHC)]
            gu_pool = (0, 1, 2, 3)
            k = 0
            order = []
            for ti in range(ntl):
                order += [(0, ti), (1, ti)]
            for sp in range(2, NHC // 2):
                order += [(sp, ti) for ti in range(ntl)]
            done_cnt = {}
            for (sp, ti) in order:
                wi = sp % 2
                if True:
                    xo, n, ho, hc, xbufs, hbuf = tiles[ti]
                    for jj in range(2):
                        j = 2 * sp + jj
                        psg, psgb = self.bank(gu_pool)
                        psu, psub = self.bank(gu_pool)
                        for gu, (pt, pb) in enumerate(((psg, psgb), (psu, psub))):
                            for kc in range(NKC):
                                T.op('pe', lambda e: e.matmul(
                                    pt[:, 0:n], lhsT=w1t[wi][:, kc, gu, jj * 128:(jj + 1) * 128],
                                    rhs=h[:, kc, ho:ho + n], start=(kc == 0), stop=(kc == NKC - 1)),
                                    reads=[w1b[wi], hbuf], writes=[pb], inc=(kc == NKC - 1))
                        si = k % 2
                        k += 1
                        T.op('act', lambda e: e.activation(out=sg[si][:, 0:n], in_=psg[:, 0:n], func=AF.Silu),
                             reads=[psgb], writes=[sgb[si]])
                        T.op('dve', lambda e: e.tensor_tensor(
                            out=hid[:, j, ho:ho + n], in0=sg[si][:, 0:n], in1=psu[:, 0:n], op=ALU.mult),
                            reads=[sgb[si], psub], writes=[hidb[j][ti]])
                if sp == 0 and ti + 1 < ntl:
                    do_pre(ti + 1)
                done_cnt[sp] = done_cnt.get(sp, 0) + 1
                if done_cnt[sp] == ntl:
                    self.bg_step(2)
                if done_cnt[sp] == ntl and sp + 2 < NHC // 2:
                    load1(sp + 2)
            fb = [[Buf('f') for _ in range(NKC)] for _ in tiles]
            nbanks = (6, 7)
            psn = [(self.ps[nbanks[ti]], self.psb[nbanks[ti]]) for ti in range(ntl)]
            pend = []
            q = 0

            def flush(keep):
                while len(pend) > keep:
                    ti_, m_, qi_, n_ = pend.pop(0)
                    pt_, pb_ = psn[ti_]
                    T.op('pe', lambda e: e.matmul(pt_[:, 0:n_], lhsT=self.meanm, rhs=sqf[qi_][:, 0:n_],
                                                  start=(m_ == 0), stop=(m_ == NKC - 1)),
                         reads=[sqfb[qi_], self.const_b], writes=[pb_], inc=True)

            order2 = [(m, ti) for m in range(NKC - 3) for ti in range(ntl)]
            for ti in range(ntl):
                order2 += [(m, ti) for m in range(NKC - 3, NKC)]
            done2 = {}

            def post(ti):
                xo, n, ho, hc, xbufs, hbuf = tiles[ti]
                pt_, pb_ = psn[ti]
                T.op('act', lambda e: e.activation(out=rstd[:, ti, 0:n], in_=pt_[:, 0:n], func=AF.Ln,
                                                   bias=self.epsc[:, 0:1], scale=1.0),
                     reads=[pb_, self.constf_b], writes=[rstdb[ti]])
                T.op('act', lambda e: e.activation(out=rstd[:, ti, 0:n], in_=rstd[:, ti, 0:n], func=AF.Exp, scale=-0.5),
                     reads=[rstdb[ti]], writes=[rstdb[ti]])
                for m in range(NKC + 1):
                    if m < NKC:
                        tb, tbb = tmp[m % 4], tmpb[m % 4]
                        T.op('dve', lambda e: e.tensor_tensor(out=tb[:, 0:n], in0=f[:, m, ho:ho + n],
                                                              in1=rstd[:, ti, 0:n], op=ALU.mult),
                             reads=[fb[ti][m], rstdb[ti]], writes=[tbb])
                    if m >= 1:
                        m1 = m - 1
                        tb1, tbb1 = tmp[m1 % 4], tmpb[m1 % 4]
                        T.op('dve', lambda e: e.tensor_tensor(out=x[:, m1, xo:xo + n], in0=x[:, m1, xo:xo + n],
                                                              in1=tb1[:, 0:n], op=ALU.add),
                             reads=[tbb1], writes=[self.xcb[b_][m1] for b_ in range(xo // 128, (xo + n) // 128)])

            for (m, ti) in order2:
                wi = m % 3
                xo, n, ho, hc, xbufs, hbuf = tiles[ti]
                psf, psfb = self.bank((0, 1, 2, 3, 4, 5))
                for kc in range(NHC):
                    T.op('pe', lambda e: e.matmul(
                        psf[:, 0:n], lhsT=w2t[wi][:, kc, :], rhs=hid[:, kc, ho:ho + n],
                        start=(kc == 0), stop=(kc == NHC - 1)),
                        reads=[w2b[wi], hidb[kc][ti]], writes=[psfb], inc=(kc == NHC - 1))
                flush(1)
                qi = q % 4
                q += 1
                T.op('act', lambda e: e.activation(out=sqf[qi][:, 0:n], in_=psf[:, 0:n], func=AF.Square),
                     reads=[psfb], writes=[sqfb[qi]])
                T.op('dve', lambda e: e.tensor_scalar(
                    out=f[:, m, ho:ho + n], in0=psf[:, 0:n], scalar1=self.coefG[:, l, s, m, cv:cv + 1],
                    scalar2=None, op0=ALU.mult), reads=[psfb, self.coef_b], writes=[fb[ti][m]])
                pend.append((ti, m, qi, n))
                done2[m] = done2.get(m, 0) + 1
                if done2[m] == ntl and m + 3 < NKC:
                    load2(m + 3)
                if done2[m] == ntl:
                    self.bg_pool = (0, 1, 2, 3, 4, 5)
                    self.bg_step(2)
                    self.bg_pool = (6, 7)
                if m == NKC - 1:
                    flush(0)
                    post(ti)
            T.barrier()

    def run_supergroup(self, kind, depth):
        T, dram = self.T, self.dram
        cfg = self.cfg
        x = self.x
        if kind == 's':
            self.bg_step(10 ** 6)
            T.barrier()
        if kind == 'p':
            nb = NPB
            cv = 0
        else:
            nb = NSB
            cv = 1
            T.dma('sp', x[:, :, :], dram['xs'], 'x', writes=[c_ for b in range(NSB) for c_ in self.xcb[b]])
        for l in range(depth):
            if kind == 'p':
                kvb = list(range(NPB))
                fullb = list(range(NPB))
                gmax = 8
            else:
                kvb = list(range(l, NSB - l))
                fullb = list(range(l + 1, NSB - 1 - l))
                gmax = 8
            if cfg['ffn1']:
                for g in make_groups(split_tiles(kvb, 4), gmax):
                    self.ffn(g, l, 0, 0, cv)
            if cfg['mixer']:
                self.mixer(kind, l, kvb, fullb, cv)
            if cfg['ffn2']:
                for g in make_groups(split_tiles(fullb, 4), gmax):
                    self.ffn(g, l, 1, 2, cv)
        if kind == 'p':
            self.bg_step(10 ** 6)
            T.dma('sp', dram['yp'], x[:, :, 0:NPB * 128], 'y', reads=[c_ for b in range(NPB) for c_ in self.xcb[b]])
        else:
            T.dma('sp', dram['ys'], x[:, :, 512:1024], 'y', reads=[c_ for b in range(4, 8) for c_ in self.xcb[b]])
        T.barrier()

    def mixer(self, kind, l, kvb, fullb, cv):
        T, nc, dram = self.T, self.nc, self.dram
        x = self.x
        is_s = (kind == 's')
        NB = NSB if is_s else NPB
        NTs = NB * 128
        with ExitStack() as SA:
            qT = self.sb(SA, 'qT', [64, 8, NTs], BF16)
            kT = self.sb(SA, 'kT', [64, 2, NTs], BF16)
            guT = self.sb(SA, 'guT', [128, 2, NTs], BF16)
            vtm = self.sb(SA, 'vtm', [128, NB, 2, 128], BF16)
            vh = self.sb(SA, 'vh', [128, NB, 256], BF16)
            pltm = self.sb(SA, 'pltm', [128, NB, 256], BF16)
            qb = [Buf('q%d' % b) for b in range(NB)]
            kb = [Buf('k%d' % b) for b in range(NB)]
            gub = [Buf('gu%d' % b) for b in range(NB)]
            vb = [Buf('v%d' % b) for b in range(NB)]
            vhb = [Buf('vh%d' % b) for b in range(NB)]
            plb = [Buf('pl%d' % b) for b in range(NB)]
            T.op('dve', lambda e: e.memset(vtm[:, :, :, 64:128], 1.0), writes=vb)
            with ExitStack() as S:
                scrs = [self.norm_scratch(S) for _ in range(2)]
                h8s = [self.sb(S, 'h8_%d' % i, [128, NKC, 512], BF16) for i in range(2)]
                wst = [self.sb(S, 'wst%d' % i, [128, NKC, 256], BF16) for i in range(2)]
                wstb = [Buf('wst0'), Buf('wst1')]
                wss = ['wia', 'wib']
                wtm = self.sb(S, 'wtm', [128, NKC, 640], BF16)
                wtmb = Buf('wtm')
                mcfb = Buf('mcf')
                if is_s:
                    rope = self.sb(S, 'rope', [128, 2, NTs], F32)
                    T.dma('sp', rope[:, :, :], dram['rope'], 'mcf', writes=[mcfb])
                    rt = [self.sb(S, 'rt%d' % i, [128, 512], F32) for i in range(4)]
                    rtb = [Buf('rt%d' % i) for i in range(4)]
                else:
                    kst = [self.sb(S, 'kst%d' % i, [128, 512], F32) for i in range(2)]
                    kstb = [Buf('kst0'), Buf('kst1')]
                    vst = [self.sb(S, 'vst%d' % i, [128, 4, 128], F32) for i in range(2)]
                    vstb = [Buf('vst0'), Buf('vst1')]
                sqg = [self.sb(S, 'sqg%d' % i, [128, 256], F32) for i in range(2)]
                sqgb = [Buf('sqg0'), Buf('sqg1')]
                ss = [self.sb(S, 'ss%d' % i, [128, 4], F32) for i in range(2)]
                ssb = [Buf('ss0'), Buf('ss1')]
                wsrc = dram['w_in'][l].rearrange("(kc p) n -> p kc n", p=128)
                stripes = list(range(6))
                tiles = split_tiles(kvb, 4)
                seq = [(ti, st) for ti in range(len(tiles)) for st in stripes]

                def loadst(i):
                    ti, st = seq[i]
                    T.dma('pool', wst[i % 2][:, :, :], wsrc[:, :, 256 * st:256 * st + 256], wss[i % 2],
                          writes=[wstb[i % 2]])

                loadst(0)
                loadst(1)
                gp = (0, 1, 2, 3, 4, 5)
                rk = 0
                hbufs = {}

                def do_prenorm(ti):
                    t = tiles[ti]
                    hb_ = Buf('h8')
                    hbufs[ti] = hb_
                    self.prenorm(t[0] * 128, len(t) * 128, [c_ for b in t for c_ in self.xcb[b]], h8s[ti % 2], 0, hb_, l, 1, cv,
                                 scrs[ti % 2])

                do_prenorm(0)
                for ti, t in enumerate(tiles):
                    n = len(t) * 128
                    xo = t[0] * 128
                    h8 = h8s[ti % 2]
                    hbuf = hbufs[ti]
                    T.dma('pool', wtm[:, :, :], wsrc[:, :, 1536:2176], 'wtm', writes=[wtmb])
                    if ti + 1 < len(tiles):
                        do_prenorm(ti + 1)
                    for st in stripes:
                        i = ti * len(stripes) + st
                        wi = i % 2

                        def grp(g):
                            pt, pb = self.bank(gp)
                            for kc in range(NKC):
                                T.op('pe', lambda e: e.matmul(
                                    pt[:, 0:n], lhsT=wst[wi][:, kc, 128 * g:128 * g + 128], rhs=h8[:, kc, 0:n],
                                    start=(kc == 0), stop=(kc == NKC - 1)),
                                    reads=[wstb[wi], hbuf], writes=[pb], inc=(kc == NKC - 1))
                            return pt, pb

                        if st <= 4:
                            if st < 4:
                                heads = (2 * st, 2 * st + 1)
                                dst, dbufs = qT, qb
                            else:
                                heads = (0, 1)
                                dst, dbufs = kT, kb
                            wb = [dbufs[b] for b in t]
                            pt, pb = grp(0)
                            if is_s:
                                pp, ppb = grp(1)
                                r0, r0b = rt[rk % 4], rtb[rk % 4]
                                r1, r1b = rt[(rk + 1) % 4], rtb[(rk + 1) % 4]
                                rk += 2
                                T.op('dve', lambda e: e.tensor_tensor(
                                    out=r0[:, 0:n], in0=pt[:, 0:n], in1=rope[:, 0, xo:xo + n], op=ALU.mult),
                                    reads=[pb, mcfb], writes=[r0b])
                                T.op('dve', lambda e: e.tensor_tensor(
                                    out=r1[:, 0:n], in0=pp[:, 0:n], in1=rope[:, 1, xo:xo + n], op=ALU.mult),
                                    reads=[ppb, mcfb], writes=[r1b])
                                for hi, head in enumerate(heads):
                                    T.op('dve', lambda e: e.tensor_tensor(
                                        out=dst[:, head, xo:xo + n], in0=r0[64 * hi:64 * hi + 64, 0:n],
                                        in1=r1[64 * hi:64 * hi + 64, 0:n], op=ALU.add),
                                        reads=[r0b, r1b], writes=wb)
                            else:
                                for hi, head in enumerate(heads):
                                    T.op('act', lambda e: e.activation(
                                        out=dst[:, head, xo:xo + n], in_=pt[64 * hi:64 * hi + 64, 0:n], func=AF.Copy),
                                        reads=[pb], writes=wb)
                                if st == 4:
                                    ks, ksb = kst[ti % 2], kstb[ti % 2]
                                    T.op('dve', lambda e: e.tensor_copy(out=ks[:, 0:n], in_=pt[:, 0:n]),
                                         reads=[pb], writes=[ksb])
                                    T.dma('sp', dram['nkT'][l][:, xo:xo + n], ks[:, 0:n], 'ko', reads=[ksb])
                        else:
                            for c in range(2):
                                pt, pb = grp(c)
                                T.op('act', lambda e: e.activation(
                                    out=guT[:, c, xo:xo + n], in_=pt[:, 0:n], func=AF.Copy),
                                    reads=[pb], writes=[gub[b] for b in t])
                        if i + 2 < len(seq):
                            loadst(i + 2)
                    for bi, b in enumerate(t):
                        pa, pab = self.bank(gp)
                        pbk, pbb = self.bank(gp)
                        for kc in range(NKC):
                            T.op('pe', lambda e: e.matmul(
                                pa[:, 0:384], lhsT=h8[:, kc, 128 * bi:128 * bi + 128], rhs=wtm[:, kc, 0:384],
                                start=(kc == 0), stop=(kc == NKC - 1)),
                                reads=[wtmb, hbuf], writes=[pab], inc=(kc == NKC - 1))
                        for kc in range(NKC):
                            T.op('pe', lambda e: e.matmul(
                                pbk[:, 0:256], lhsT=h8[:, kc, 128 * bi:128 * bi + 128], rhs=wtm[:, kc, 384:640],
                                start=(kc == 0), stop=(kc == NKC - 1)),
                                reads=[wtmb, hbuf], writes=[pbb], inc=(kc == NKC - 1))
                        T.op('act', lambda e: e.activation(
                            out=vtm[:, b, :, 0:64], in_=pa[:, 0:128].rearrange("p (h d) -> p h d", d=64), func=AF.Copy),
                            reads=[pab], writes=[vb[b]])
                        if not is_s:
                            vs, vsb = vst[ti % 2], vstb[ti % 2]
                            T.op('dve', lambda e: e.tensor_copy(out=vs[:, bi, :], in_=pa[:, 0:128]),
                                 reads=[pab], writes=[vsb])
                        si = bi % 2
                        T.op('act', lambda e: e.activation(out=sqg[si][:, :], in_=pa[:, 128:384], func=AF.Square),
                             reads=[pab], writes=[sqgb[si]])
                        T.op('dve', lambda e: e.tensor_reduce(
                            out=ss[si][:, :], in_=sqg[si][:, :].rearrange("p (h d) -> p h d", d=64),
                            axis=AX.X, op=ALU.add), reads=[sqgb[si]], writes=[ssb[si]])
                        T.op('act', lambda e: e.activation(
                            out=ss[si][:, :], in_=ss[si][:, :], func=AF.Ln, bias=self.epsc[:, 0:1],
                            scale=1.0 / 64.0), reads=[ssb[si], self.constf_b], writes=[ssb[si]])
                        T.op('act', lambda e: e.activation(out=ss[si][:, :], in_=ss[si][:, :], func=AF.Exp, scale=-0.5),
                     reads=[ssb[si]], writes=[ssb[si]])
                        T.op('dve', lambda e: e.tensor_tensor(
                            out=vh[:, b, :].rearrange("p (h d) -> p h d", d=64),
                            in0=pa[:, 128:384].rearrange("p (h d) -> p h d", d=64),
                            in1=ss[si][:, :].unsqueeze(2).to_broadcast([128, 4, 64]), op=ALU.mult),
                            reads=[pab, ssb[si]], writes=[vhb[b]])
                        T.op('act', lambda e: e.activation(out=pltm[:, b, :], in_=pbk[:, 0:256], func=AF.Copy),
                             reads=[pbb], writes=[plb[b]])
                    if not is_s:
                        T.dma('sp', dram['nv'][l][xo:xo + n, :].rearrange("(b p) c -> p b c", p=128),
                              vs[:, 0:len(t), :], 'vo', reads=[vsb])
                T.barrier()
            with ExitStack() as S:
                sqf = [self.sb(S, 'sqfm%d' % i, [128, 512], BF16) for i in range(4)]
                sqfb = [Buf('sqfm%d' % i) for i in range(4)]
                rstdm = self.sb(S, 'rstdm', [128, 512], F32)
                rstdmb = Buf('rstdm')
                tmpm = [self.sb(S, 'tmpm%d' % i, [128, 512], F32) for i in range(4)]
                tmpmb = [Buf('tmpm%d' % i) for i in range(4)]
                qk = 0
                mcb = Buf('mc2')
                mcfb = Buf('mcf2')
                masks = self.sb(S, 'masks', [128, 4, 512], BF16)
                PM = self.sb(S, 'PM', [128, 36, 128], BF16)
                wsT = self.sb(S, 'wsT', [128, 4, 128], BF16)
                bbc = self.sb(S, 'bbc', [64, 4, 128], F32)
                T.dma('pool', PM[:, 0:20, :], dram['PM'], 'mc', writes=[mcb])
                T.dma('pool', wsT[:, :, :], dram['wsT'][l], 'mc', writes=[mcb])
                T.dma('sp', bbc[:, :, :], dram['bbc'][l], 'mcf', writes=[mcfb])
                if is_s:
                    ck = self.sb(S, 'ck', [64, 2, 512], BF16)
                    cvt = self.sb(S, 'cvt', [128, 4, 2, 128], BF16)
                    T.dma('pool', masks[:, 0:2, :], dram['masks'], 'mc', writes=[mcb])
                    T.dma('pool', masks[:, 2:4, :], dram['masko'], 'mc', writes=[mcb])
                    T.dma('pool', PM[:, 20:36, :], dram['PMo'], 'mc', writes=[mcb])
                    T.dma('pool', ck[:, :, :], dram['ckT'][l], 'mc', writes=[mcb])
                    T.dma('pool', cvt[:, :, :, :], dram['cv'][l], 'mc', writes=[mcb])
                mcb.w = ('mc', T.dsem['mc'][1])
                mixed = self.sb(S, 'mixed', [128, 8, 512], BF16)
                mixb = [Buf('mix%d' % j) for j in range(8)]
                PT = [self.sb(S, 'PT%d' % i, [128, 512], BF16) for i in range(4)]
                PTb = [Buf('PT%d' % i) for i in range(4)]
                den = [self.sb(S, 'den%d' % i, [64, 512], F32) for i in range(2)]
                denb = [Buf('den0'), Buf('den1')]
                zt = [self.sb(S, 'zt%d' % i, [128, 512], F32) for i in range(2)]
                ztb = [Buf('zt0'), Buf('zt1')]
                dbf = [self.sb(S, 'dbf%d' % i, [64, 512], BF16) for i in range(2)]
                dbfb = [Buf('dbf0'), Buf('dbf1')]
                wot = [self.sb(S, 'wot%d' % i, [128, NKC, 512], BF16) for i in range(2)]
                wotb = [Buf('wot%d' % i) for i in range(2)]
                wos = ['woa', 'wob']
                f = self.sb(S, 'fm', [128, NKC, 512], F32)
                wosrc = dram['w_out'][l].rearrange("(kc p) n -> p kc n", p=128)
                tiles = split_tiles(fullb, 4)
                seqw = [(ti, hf) for ti in range(len(tiles)) for hf in range(2)]

                def loadwo(i):
                    ti, hf = seqw[i]
                    T.dma('pool', wot[i % 2][:, :, :], wosrc[:, :, 512 * hf:512 * hf + 512], wos[i % 2],
                          writes=[wotb[i % 2]])

                for i in range(min(2, len(seqw))):
                    loadwo(i)
                uk = 0
                pk = 0
                for ti, t in enumerate(tiles):
                    n = len(t) * 128
                    xo = t[0] * 128
                    xbufs = [c_ for b in t for c_ in self.xcb[b]]
                    units = []
                    for bi, b in enumerate(t):
                        for hk in range(2):
                            keys = []
                            if is_s:
                                mp = masks[:, 2, :] if b == 4 else masks[:, 0, :]
                                mn = masks[:, 3, :] if b == 7 else masks[:, 1, :]
                                keys.append((kT[:, hk, 128 * (b - 1):128 * b], [kb[b - 1]],
                                             vtm[:, b - 1, hk, :], [vb[b - 1]], mp))
                                keys.append((kT[:, hk, 128 * b:128 * (b + 1)], [kb[b]],
                                             vtm[:, b, hk, :], [vb[b]], None))
                                keys.append((kT[:, hk, 128 * (b + 1):128 * (b + 2)], [kb[b + 1]],
                                             vtm[:, b + 1, hk, :], [vb[b + 1]], mn))
                                for c in range(4):
                                    keys.append((ck[:, hk, 128 * c:128 * c + 128], [mcb],
                                                 cvt[:, c, hk, :], [mcb], None))
                            else:
                                sb0 = (b // 2) * 2
                                for kb_ in (sb0, sb0 + 1):
                                    keys.append((kT[:, hk, 128 * kb_:128 * kb_ + 128], [kb[kb_]],
                                                 vtm[:, kb_, hk, :], [vb[kb_]], None))
                            units.append(dict(bi=bi, b=b, hk=hk, keys=keys, pso=None))
                    items = [(u, ki) for u in units for ki in range(len(u['keys']))]

                    def emit_scores(u, ki):
                        kap, kbufs, vap, vbufs, mask = u['keys'][ki]
                        b, hk = u['b'], u['hk']
                        if u['pso'] is None:
                            u['pso'] = self.bank((0, 1, 2, 3))
                        rhs_q = qT[:, 4 * hk:4 * hk + 4, 128 * b:128 * b + 128]
                        psc, pscb = self.bank((4, 5, 6, 7))
                        T.op('pe', lambda e: e.matmul(
                            psc[:, :].rearrange("p (g t) -> p g t", g=4), lhsT=kap, rhs=rhs_q,
                            start=True, stop=(mask is None)),
                            reads=kbufs + [qb[b]], writes=[pscb], inc=(mask is None))
                        if mask is not None:
                            T.op('pe', lambda e: e.matmul(
                                psc[:, :], lhsT=self.ident, rhs=mask, start=False, stop=True),
                                reads=[mcb, self.const_b], writes=[pscb], inc=True)
                        return psc, pscb

                    def emit_exp_pv(u, ki, psc, pscb):
                        nonlocal pk, uk
                        kap, kbufs, vap, vbufs, mask = u['keys'][ki]
                        nk = len(u['keys'])
                        pso, psob = u['pso']
                        pi = pk % 4
                        pk += 1
                        T.op('act', lambda e: e.activation(
                            out=PT[pi][:, :], in_=psc[:, :], func=AF.Exp, scale=SCALE),
                            reads=[pscb], writes=[PTb[pi]])
                        T.op('pe', lambda e: e.matmul(
                            pso[:, :], lhsT=vap, rhs=PT[pi][:, :], start=(ki == 0), stop=(ki == nk - 1)),
                            reads=vbufs + [PTb[pi]], writes=[psob], inc=(ki == nk - 1))
                        if ki == nk - 1:
                            hk, bi = u['hk'], u['bi']
                            di = uk % 2
                            uk += 1
                            T.op('dve', lambda e: e.tensor_tensor(
                                out=den[di][:, :].rearrange("p (g t) -> p g t", g=4),
                                in0=pso[64:128, :].rearrange("p (g t) -> p g t", g=4),
                                in1=self.esink[:, l, 4 * hk:4 * hk + 4].unsqueeze(2).to_broadcast([64, 4, 128]),
                                op=ALU.add), reads=[psob, self.constf_b], writes=[denb[di]])
                            T.op('act', lambda e: e.activation(out=den[di][:, :], in_=den[di][:, :], func=AF.Ln),
                                 reads=[denb[di]], writes=[denb[di]])
                            T.op('act', lambda e: e.activation(out=den[di][:, :], in_=den[di][:, :], func=AF.Exp,
                                                               scale=-1.0),
                                 reads=[denb[di]], writes=[denb[di]])
                            for par in range(2):
                                T.op('dve', lambda e: e.tensor_tensor(
                                    out=mixed[64 * par:64 * par + 64, 2 * hk:2 * hk + 2, 128 * bi:128 * bi + 128],
                                    in0=pso[0:64, :].rearrange("p (gg par t) -> p gg par t", par=2, t=128)[:, :, par, :],
                                    in1=den[di][:, :].rearrange("p (gg par t) -> p gg par t", par=2, t=128)[:, :, par, :],
                                    op=ALU.mult),
                                    reads=[psob, denb[di]], writes=mixb[2 * hk:2 * hk + 2])

                    prev = None
                    for (u, ki) in items:
                        sc = emit_scores(u, ki)
                        if prev is not None:
                            emit_exp_pv(*prev)
                        prev = (u, ki, sc[0], sc[1])
                    if prev is not None:
                        emit_exp_pv(*prev)
                    for hh in range(4):
                        psz, pszb = self.bank((4, 5, 6, 7))
                        for bi, b in enumerate(t):
                            T.op('pe', lambda e: e.matmul(
                                psz[0:64, 128 * bi:128 * bi + 128], lhsT=vh[:, b, 64 * hh:64 * hh + 64],
                                rhs=wsT[:, hh, :], start=True, stop=True),
                                reads=[vhb[b], mcb], writes=[pszb], inc=(bi == len(t) - 1))
                        zi = hh % 2
                        po = 64 * (hh % 2)
                        T.op('dve', lambda e: e.tensor_tensor(
                            out=zt[zi][po:po + 64, 0:n].rearrange("p (b t) -> p b t", t=128),
                            in0=psz[0:64, 0:n].rearrange("p (b t) -> p b t", t=128),
                            in1=bbc[:, hh, :].unsqueeze(1).to_broadcast([64, len(t), 128]), op=ALU.add),
                            reads=[pszb, mcfb], writes=[ztb[zi]])
                        T.op('dve', lambda e: e.tensor_tensor(
                            out=mixed[po:po + 64, 4 + hh // 2, 0:n], in0=zt[zi][po:po + 64, 0:n],
                            in1=guT[po:po + 64, hh // 2, xo:xo + n], op=ALU.mult),
                            reads=[ztb[zi]] + [gub[b] for b in t], writes=[mixb[4 + hh // 2]])
                    for g in range(4):
                        psd, psdb = self.bank((4, 5, 6, 7))
                        for bi, b in enumerate(t):
                            if is_s:
                                if b == 4:
                                    nbrs = [(b - 1, 20 + g), (b, 24 + g), (b + 1, 16 + g)]
                                elif b == 7:
                                    nbrs = [(b - 1, 0 + g), (b, 28 + g), (b + 1, 32 + g)]
                                else:
                                    nbrs = [(b - 1, 0 + g), (b, 4 + g), (b + 1, 16 + g)]
                            else:
                                if b % 2 == 0:
                                    nbrs = [(b, 8 + g), (b + 1, 16 + g)]
                                else:
                                    nbrs = [(b - 1, 0 + g), (b, 12 + g)]
                            for ni, (nbk, pmi) in enumerate(nbrs):
                                nn = len(nbrs)
                                T.op('pe', lambda e: e.matmul(
                                    psd[0:64, 128 * bi:128 * bi + 128], lhsT=pltm[:, nbk, 64 * g:64 * g + 64],
                                    rhs=PM[:, pmi, :], start=(ni == 0), stop=(ni == nn - 1)),
                                    reads=[plb[nbk], mcb], writes=[psdb],
                                    inc=(bi == len(t) - 1 and ni == len(nbrs) - 1))
                        dj = g % 2
                        po = 64 * (g % 2)
                        T.op('act', lambda e: e.activation(out=dbf[dj][:, 0:n], in_=psd[0:64, 0:n], func=AF.Copy),
                             reads=[psdb], writes=[dbfb[dj]])
                        psp, pspb = self.bank((4, 5, 6, 7))
                        T.op('pe', lambda e: e.matmul(
                            psp[0:64, 0:n], lhsT=self.wpool[:, l, g, :], rhs=dbf[dj][:, 0:n], start=True, stop=True),
                            reads=[dbfb[dj], self.const_b], writes=[pspb], inc=True)
                        T.op('dve', lambda e: e.tensor_scalar(
                            out=mixed[po:po + 64, 6 + g // 2, 0:n], in0=psp[0:64, 0:n],
                            scalar1=self.pscale[:, l, g:g + 1], scalar2=None, op0=ALU.mult),
                            reads=[pspb, self.constf_b], writes=[mixb[6 + g // 2]])
                    fb = [Buf('fm%d' % m) for m in range(NKC)]
                    nbk = 7 - (ti % 2)
                    pnt, pnb = self.ps[nbk], self.psb[nbk]
                    pend = []

                    def flush(keep):
                        while len(pend) > keep:
                            m_, qi_ = pend.pop(0)
                            T.op('pe', lambda e: e.matmul(pnt[:, 0:n], lhsT=self.meanm, rhs=sqf[qi_][:, 0:n],
                                                          start=(m_ == 0), stop=(m_ == NKC - 1)),
                                 reads=[sqfb[qi_], self.const_b], writes=[pnb], inc=True)

                    for m in range(NKC):
                        i = ti * 2 + m // 4
                        wi = i % 2
                        psf, psfb = self.bank((0, 1, 2, 3))
                        for kc in range(NKC):
                            T.op('pe', lambda e: e.matmul(
                                psf[:, 0:n], lhsT=wot[wi][:, kc, 128 * (m % 4):128 * (m % 4) + 128],
                                rhs=mixed[:, kc, 0:n], start=(kc == 0), stop=(kc == NKC - 1)),
                                reads=[wotb[wi], mixb[kc]], writes=[psfb], inc=(kc == NKC - 1))
                        flush(1)
                        qi = qk % 4
                        qk += 1
                        T.op('act', lambda e: e.activation(out=sqf[qi][:, 0:n], in_=psf[:, 0:n], func=AF.Square),
                             reads=[psfb], writes=[sqfb[qi]])
                        T.op('dve', lambda e: e.tensor_scalar(
                            out=f[:, m, 0:n], in0=psf[:, 0:n], scalar1=self.coefG[:, l, 1, m, cv:cv + 1],
                            scalar2=None, op0=ALU.mult), reads=[psfb, self.coef_b], writes=[fb[m]])
                        pend.append((m, qi))
                        if m % 4 == 3 and i + 2 < len(seqw):
                            loadwo(i + 2)
                    flush(0)
                    T.op('act', lambda e: e.activation(out=rstdm[:, 0:n], in_=pnt[:, 0:n], func=AF.Ln,
                                                       bias=self.epsc[:, 0:1], scale=1.0),
                         reads=[pnb, self.constf_b], writes=[rstdmb])
                    T.op('act', lambda e: e.activation(out=rstdm[:, 0:n], in_=rstdm[:, 0:n], func=AF.Exp, scale=-0.5),
                     reads=[rstdmb], writes=[rstdmb])
                    for m in range(NKC + 1):
                        if m < NKC:
                            tb, tbb = tmpm[m % 4], tmpmb[m % 4]
                            T.op('dve', lambda e: e.tensor_tensor(out=tb[:, 0:n], in0=f[:, m, 0:n],
                                                                  in1=rstdm[:, 0:n], op=ALU.mult),
                                 reads=[fb[m], rstdmb], writes=[tbb])
                        if m >= 1:
                            m1 = m - 1
                            tb1, tbb1 = tmpm[m1 % 4], tmpmb[m1 % 4]
                            T.op('dve', lambda e: e.tensor_tensor(out=x[:, m1, xo:xo + n], in0=x[:, m1, xo:xo + n],
                                                                  in1=tb1[:, 0:n], op=ALU.add),
                                 reads=[tbb1], writes=[self.xcb[b_][m1] for b_ in t])
                T.barrier()


def _pool_mats():
    PM = np.zeros((128, 20, 128), np.float32)
    tp = np.arange(128)[:, None]
    t = np.arange(128)[None, :]
    for g, w in enumerate(POOL_WINDOWS):
        hf = w // 2
        eye = (tp == t).astype(np.float32)
        same = ((tp >= t - hf) & (tp < t + hf)).astype(np.float32)
        prev = ((tp - 128) >= (t - hf)).astype(np.float32)
        nxt = ((tp + 128) < (t + hf)).astype(np.float32)
        PM[:, 0 + g, :] = prev / w
        PM[:, 4 + g, :] = same / w - eye
        cnt_first = (np.minimum(t + hf, 128 + hf) - np.maximum(t - hf, 0)).astype(np.float32)
        PM[:, 8 + g, :] = same / cnt_first - eye
        cnt_last = (np.minimum(t + hf, 128) - (t - hf)).astype(np.float32)
        PM[:, 12 + g, :] = same / cnt_last - eye
        PM[:, 16 + g, :] = nxt / w
    return PM


def _win_cols():
    def perm64(base):
        idx = np.arange(64)
        half = idx % 32
        src = np.where(half < 16, idx + 16, idx - 16)
        return base + src
    cols = []
    for st in range(4):
        cols.append(np.arange(128) + 2 * st * 64)
        cols.append(perm64(2 * st * 64))
        cols.append(perm64((2 * st + 1) * 64))
    cols.append(np.arange(128) + 512)
    cols.append(perm64(512))
    cols.append(perm64(512 + 64))
    for hh in range(4):
        cols.append(np.arange(64) + 768 + hh * 64)
    cols.append(np.arange(128) + 640)
    cols.append(np.arange(256) + 1024)
    cols.append(np.arange(256) + 1280)
    return np.concatenate(cols)


_NC_CACHE = {}


def _get_nc(cfg):
    key = tuple(sorted(cfg.items()))
    if key not in _NC_CACHE:
        _NC_CACHE[key] = Builder(dict(cfg)).build()
    return _NC_CACHE[key]


def kernel(x_prompt, x_sample, cache_k, cache_v, c, c_ctx, w_mod, b_mod, norm_w, w_in, w_out,
           attn_sink, w_spatial, b_spatial, w_pool, pool_scale, ffn_w1, ffn_w2, _cfg=None):
    cfg = dict(CFG) if _cfg is None else dict(_cfg)
    f32 = np.float32
    A = lambda a: np.ascontiguousarray(np.asarray(a, dtype=f32))
    x_prompt, x_sample, cache_k, cache_v = A(x_prompt), A(x_sample), A(cache_k), A(cache_v)
    c, c_ctx = A(c), A(c_ctx)
    shared = {}
    shared['w_mod'] = A(w_mod)
    shared['w1'] = A(ffn_w1)
    shared['w2'] = A(ffn_w2)
    shared['w_in'] = A(np.asarray(w_in, f32)[:, :, _win_cols()])
    shared['w_out'] = A(w_out)
    shared['nw'] = A(np.asarray(norm_w, f32).reshape(L, 6, NKC, 128).transpose(3, 0, 1, 2))
    shared['bmod'] = A(np.asarray(b_mod, f32).reshape(L, 72, 128).transpose(2, 0, 1))
    shared['wsT'] = A(np.asarray(w_spatial, f32).transpose(0, 3, 1, 2))
    shared['bbc'] = A(np.broadcast_to(np.asarray(b_spatial, f32)[:, None, :, :], (L, 64, 4, 128)))
    shared['sinkb'] = A(np.broadcast_to(np.asarray(attn_sink, f32)[None], (64, L, 8)))
    shared['pscale'] = A(np.asarray(pool_scale, f32).reshape(L, 4, 64).transpose(2, 0, 1))
    shared['wpool'] = A(np.asarray(w_pool, f32).transpose(2, 0, 1, 3))
    cmat = np.zeros((128, 320), f32)
    cmat[:, 0:128] = np.eye(128, dtype=f32)
    cmat[:, 128:256] = 1.0 / 1024.0
    cmat[:, 256:320] = 1.0
    shared['cmat'] = cmat
    kk = np.arange(128)[:, None]
    qq = np.arange(128)[None, :]
    maskL = np.where(kk >= qq, 0.0, NEG).astype(f32)
    maskU = np.where(kk <= qq, 0.0, NEG).astype(f32)
    dead = np.full((128, 128), NEG, f32)
    tile4 = lambda m: np.tile(m, (1, 4))
    shared['masks'] = A(np.stack([tile4(maskL), tile4(maskU)], axis=1))
    PM = _pool_mats()
    shared['PM'] = PM
    nf = 16
    inv = 10000.0 ** (-np.arange(nf, dtype=np.float64) / nf)

    in_maps = []
    for core in range(NCORES):
        m = dict(shared)
        xp = x_prompt[4 * core:4 * core + 4].reshape(NPB * 128, D)
        m['xp'] = A(xp.T.reshape(NKC, 128, NPB * 128).transpose(1, 0, 2))
        bidx = core // 4
        start = 512 * (core % 4)
        win = np.zeros((NSB * 128, D), f32)
        lo, hi = start - 512, start + 1024
        slo, shi = max(lo, 0), min(hi, 2048)
        win[slo - lo:shi - lo] = x_sample[bidx, slo:shi]
        m['xs'] = A(win.T.reshape(NKC, 128, NSB * 128).transpose(1, 0, 2))
        m['cT'] = A(np.stack([c_ctx, c[bidx]], axis=1).reshape(NKC, 128, 2).transpose(1, 0, 2))
        pos = np.clip(np.arange(lo, hi), 0, 2047)
        row = (pos // 64).astype(np.float64)
        col = (pos % 64).astype(np.float64)
        ang_r = row[None, :] * inv[:, None]
        ang_c = col[None, :] * inv[:, None]
        cosr, sinr, cosc, sinc = np.cos(ang_r), np.sin(ang_r), np.cos(ang_c), np.sin(ang_c)
        Ct = np.concatenate([cosr, cosr, cosc, cosc] * 2, axis=0).astype(f32)
        St = np.concatenate([-sinr, sinr, -sinc, sinc] * 2, axis=0).astype(f32)
        m['rope'] = A(np.stack([Ct, St], axis=1))
        first = (core % 4 == 0)
        last = (core % 4 == 3)
        mP = tile4(dead) if first else tile4(maskL)
        mN = tile4(dead) if last else tile4(maskU)
        m['masko'] = A(np.stack([mP, mN], axis=1))
        PMo = np.zeros((128, 16, 128), f32)
        PMo[:, 0:4] = 0.0 if first else PM[:, 0:4]
        PMo[:, 4:8] = PM[:, 8:12] if first else PM[:, 4:8]
        PMo[:, 8:12] = PM[:, 12:16] if last else PM[:, 4:8]
        PMo[:, 12:16] = 0.0 if last else PM[:, 16:20]
        m['PMo'] = A(PMo)
        m['ckT'] = A(cache_k[bidx].transpose(0, 3, 2, 1))
        cva = np.ones((L, 128, 4, 2, 128), f32)
        cva[..., 0:64] = cache_v[bidx].reshape(L, 4, 128, 2, 64).transpose(0, 2, 1, 3, 4)
        m['cv'] = cva
        in_maps.append(m)

    nc = _get_nc(cfg)
    res = run_bass_kernel_spmd(nc, in_maps, core_ids=list(range(NCORES)))
    y_prompt = np.zeros((32, 256, D), f32)
    y_sample = np.zeros((2, 2048, D), f32)
    nk = np.zeros((32, L, 256, 2, 64), f32)
    nv = np.zeros((32, L, 256, 2, 64), f32)
    for core in range(NCORES):
        r = res.results[core]
        yp = np.asarray(r['yp']).transpose(1, 0, 2).reshape(D, NPB * 128).T
        y_prompt[4 * core:4 * core + 4] = yp.reshape(4, 256, D)
        ys = np.asarray(r['ys']).transpose(1, 0, 2).reshape(D, 512).T
        y_sample[core // 4, 512 * (core % 4):512 * (core % 4) + 512] = ys
        k_ = np.asarray(r['nkT']).reshape(L, 2, 64, NPB * 128)
        nk[4 * core:4 * core + 4] = k_.transpose(3, 0, 1, 2).reshape(4, 256, L, 2, 64).transpose(0, 2, 1, 3, 4)
        v_ = np.asarray(r['nv'])
        nv[4 * core:4 * core + 4] = v_.reshape(L, 4, 256, 2, 64).transpose(1, 0, 2, 3, 4)
    return (y_prompt, y_sample, nk, nv)
```
